# Optimizing a Trainium2 kernel written in Bass

```python
import jax, jax.numpy as jnp
from jax import lax
import numpy as np

D_MODEL = 1024
BATCH = 8
SEQ = 4096
DEPTH = 4
DEC_BATCH = 8
DEC_SEQ = 16
PAST_LEN = 2048

CHUNK = 64
N_PREV_CHUNKS = 8
WINDOW = N_PREV_CHUNKS * CHUNK
BAND = (N_PREV_CHUNKS + 1) * CHUNK
N_HEADS = 8
HEAD_DIM = 64
ATTN_W = N_HEADS * HEAD_DIM
CONV_W = 512
CONV_K = 3
REL_CLIP = 128
N_REL = 2 * REL_CLIP + 1
D_FF = 2816
EPS = 1e-6
NEG_INF = -1e30
IN_COLS = 3 * CONV_W + 3 * ATTN_W + 2 * D_MODEL

kernel_name = "hybrid_shortconv_chunkband_macaron_step"


def _rmsnorm(x, g):
    xf = x.astype(jnp.float32)
    y = xf * lax.rsqrt(jnp.mean(xf * xf, axis=-1, keepdims=True) + EPS)
    return y.astype(x.dtype) * g


def _swiglu(h, w_gu, w_down):
    gu = h @ w_gu
    g, u = jnp.split(gu, 2, axis=-1)
    return (jax.nn.silu(g) * u) @ w_down


def _short_conv(cb, cc, cv, conv_prev, w_conv):
    u = cc * cv
    up = jnp.concatenate([conv_prev, u], axis=1)
    t = u.shape[1]
    y = w_conv[0] * up[:, 0:t]
    for j in range(1, CONV_K):
        y = y + w_conv[j] * up[:, j:j + t]
    return cb * y, up[:, -(CONV_K - 1):]


def _band_attention(q, k, v, q_pos, k_pos, rel_bias):
    s = jnp.einsum('bqhd,bkhd->bhqk', q, k).astype(jnp.float32) * (HEAD_DIM ** -0.5)
    dist = q_pos[:, None] - k_pos[None, :]
    idx = jnp.clip(dist, -REL_CLIP, REL_CLIP) + REL_CLIP
    s = s + rel_bias[:, idx].astype(jnp.float32)[None]
    qc = jnp.floor_divide(q_pos, CHUNK)[:, None]
    kc = jnp.floor_divide(k_pos, CHUNK)[None, :]
    valid = (k_pos[None, :] >= 0) & (kc <= qc) & (kc >= qc - N_PREV_CHUNKS)
    s = jnp.where(valid[None, None], s, NEG_INF)
    p = jax.nn.softmax(s, axis=-1).astype(v.dtype)
    return jnp.einsum('bhqk,bkhd->bqhd', p, v)


def _prompt_attention(q, k, v, rel_bias):
    b, s, h, d = q.shape
    nc = s // CHUNK
    pad = N_PREV_CHUNKS * CHUNK
    kp = jnp.pad(k, ((0, 0), (pad, 0), (0, 0), (0, 0)))
    vp = jnp.pad(v, ((0, 0), (pad, 0), (0, 0), (0, 0)))

    def one_chunk(c):
        qs = c * CHUNK
        qb = lax.dynamic_slice_in_dim(q, qs, CHUNK, axis=1)
        kb = lax.dynamic_slice_in_dim(kp, qs, BAND, axis=1)
        vb = lax.dynamic_slice_in_dim(vp, qs, BAND, axis=1)
        q_pos = qs + jnp.arange(CHUNK, dtype=jnp.int32)
        k_pos = qs - pad + jnp.arange(BAND, dtype=jnp.int32)
        return _band_attention(qb, kb, vb, q_pos, k_pos, rel_bias)

    out = lax.map(one_chunk, jnp.arange(nc, dtype=jnp.int32))
    out = jnp.transpose(out, (1, 0, 2, 3, 4)).reshape(b, s, h, d)
    rows = min(WINDOW, s)
    return out, k[:, -rows:], v[:, -rows:]


def _sample_attention(q, k, v, ck, cv, rel_bias):
    t = q.shape[1]
    r = ck.shape[1]
    k_all = jnp.concatenate([ck, k], axis=1)
    v_all = jnp.concatenate([cv, v], axis=1)
    q_pos = PAST_LEN + jnp.arange(t, dtype=jnp.int32)
    k_pos = jnp.concatenate([PAST_LEN - r + jnp.arange(r, dtype=jnp.int32), q_pos])
    out = _band_attention(q, k_all, v_all, q_pos, k_pos, rel_bias)
    return out, k, v


def _mixer(h, conv_prev, attn_fn, w_in, b_gate, w_conv, w_conv_out, w_attn_out, w_o):
    b, t, _ = h.shape
    z = h @ w_in
    splits = [CONV_W, 2 * CONV_W, 3 * CONV_W, 3 * CONV_W + ATTN_W,
              3 * CONV_W + 2 * ATTN_W, 3 * CONV_W + 3 * ATTN_W]
    cb, cc, cv, q, k, v, g = jnp.split(z, splits, axis=-1)
    g = jax.nn.sigmoid(g + b_gate)
    g_conv, g_attn = jnp.split(g, 2, axis=-1)
    y_conv, conv_new = _short_conv(cb, cc, cv, conv_prev, w_conv)
    q = q.reshape(b, t, N_HEADS, HEAD_DIM)
    k = k.reshape(b, t, N_HEADS, HEAD_DIM)
    v = v.reshape(b, t, N_HEADS, HEAD_DIM)
    y_attn, k_new, v_new = attn_fn(q, k, v)
    y_attn = y_attn.reshape(b, t, ATTN_W)
    m = g_conv * (y_conv @ w_conv_out) + g_attn * (y_attn @ w_attn_out)
    return m @ w_o, conv_new, k_new, v_new


def setup_inputs(seed: int = 0) -> dict:
    key = jax.random.key(seed)
    ks = jax.random.split(key, 24)
    f32 = jnp.float32
    kv_rows = min(WINDOW, PAST_LEN)

    def nrm(k, shape, scale):
        return jax.random.normal(k, shape, f32) * scale

    def gain(k, shape):
        return 1.0 + 0.02 * jax.random.normal(k, shape, f32)

    return {
        "x_prompt": nrm(ks[0], (BATCH, SEQ, D_MODEL), 1.0),
        "x_sample": nrm(ks[1], (DEC_BATCH, DEC_SEQ, D_MODEL), 1.0),
        "cache_k": nrm(ks[2], (DEPTH, DEC_BATCH, kv_rows, N_HEADS, HEAD_DIM), 1.0),
        "cache_v": nrm(ks[3], (DEPTH, DEC_BATCH, kv_rows, N_HEADS, HEAD_DIM), 1.0),
        "state_conv": nrm(ks[4], (DEPTH, DEC_BATCH, CONV_K - 1, CONV_W), 1.0),
        "norm_ffn1": gain(ks[5], (DEPTH, D_MODEL)),
        "w_ffn1_gu": nrm(ks[6], (DEPTH, D_MODEL, 2 * D_FF), D_MODEL ** -0.5),
        "w_ffn1_down": nrm(ks[7], (DEPTH, D_FF, D_MODEL), D_FF ** -0.5),
        "norm_mix": gain(ks[8], (DEPTH, D_MODEL)),
        "w_in": nrm(ks[9], (DEPTH, D_MODEL, IN_COLS), D_MODEL ** -0.5),
        "b_gate": nrm(ks[10], (DEPTH, 2 * D_MODEL), 0.02),
        "w_conv": nrm(ks[11], (DEPTH, CONV_K, CONV_W), CONV_K ** -0.5),
        "rel_bias": nrm(ks[12], (DEPTH, N_HEADS, N_REL), 0.1),
        "w_conv_out": nrm(ks[13], (DEPTH, CONV_W, D_MODEL), CONV_W ** -0.5),
        "w_attn_out": nrm(ks[14], (DEPTH, ATTN_W, D_MODEL), ATTN_W ** -0.5),
        "w_o": nrm(ks[15], (DEPTH, D_MODEL, D_MODEL), D_MODEL ** -0.5),
        "norm_ffn2": gain(ks[16], (DEPTH, D_MODEL)),
        "w_ffn2_gu": nrm(ks[17], (DEPTH, D_MODEL, 2 * D_FF), D_MODEL ** -0.5),
        "w_ffn2_down": nrm(ks[18], (DEPTH, D_FF, D_MODEL), D_FF ** -0.5),
        "norm_final": gain(ks[19], (D_MODEL,)),
    }


def reference(x_prompt, x_sample, cache_k, cache_v, state_conv,
              norm_ffn1, w_ffn1_gu, w_ffn1_down, norm_mix, w_in, b_gate, w_conv,
              rel_bias, w_conv_out, w_attn_out, w_o, norm_ffn2, w_ffn2_gu, w_ffn2_down,
              norm_final):
    xp = x_prompt
    xs = x_sample
    conv_prev_p = jnp.zeros((xp.shape[0], CONV_K - 1, CONV_W), xp.dtype)
    kp_list, vp_list, cp_list = [], [], []
    ks_list, vs_list, cs_list = [], [], []
    for l in range(DEPTH):
        rb = rel_bias[l]
        xp = xp + 0.5 * _swiglu(_rmsnorm(xp, norm_ffn1[l]), w_ffn1_gu[l], w_ffn1_down[l])
        xs = xs + 0.5 * _swiglu(_rmsnorm(xs, norm_ffn1[l]), w_ffn1_gu[l], w_ffn1_down[l])
        mp, cnp, knp, vnp = _mixer(
            _rmsnorm(xp, norm_mix[l]), conv_prev_p,
            lambda q, k, v: _prompt_attention(q, k, v, rb),
            w_in[l], b_gate[l], w_conv[l], w_conv_out[l], w_attn_out[l], w_o[l])
        ck_l = cache_k[l]
        cv_l = cache_v[l]
        ms, cns, kns, vns = _mixer(
            _rmsnorm(xs, norm_mix[l]), state_conv[l],
            lambda q, k, v: _sample_attention(q, k, v, ck_l, cv_l, rb),
            w_in[l], b_gate[l], w_conv[l], w_conv_out[l], w_attn_out[l], w_o[l])
        xp = xp + mp
        xs = xs + ms
        xp = xp + 0.5 * _swiglu(_rmsnorm(xp, norm_ffn2[l]), w_ffn2_gu[l], w_ffn2_down[l])
        xs = xs + 0.5 * _swiglu(_rmsnorm(xs, norm_ffn2[l]), w_ffn2_gu[l], w_ffn2_down[l])
        kp_list.append(knp); vp_list.append(vnp); cp_list.append(cnp)
        ks_list.append(kns); vs_list.append(vns); cs_list.append(cns)
    y_prompt = _rmsnorm(xp, norm_final)
    y_sample = _rmsnorm(xs, norm_final)
    new_k_prompt = jnp.stack(kp_list, axis=0)
    new_v_prompt = jnp.stack(vp_list, axis=0)
    new_conv_prompt = jnp.stack(cp_list, axis=0)
    new_k_sample = jnp.stack(ks_list, axis=0)
    new_v_sample = jnp.stack(vs_list, axis=0)
    new_conv_sample = jnp.stack(cs_list, axis=0)
    return (y_prompt, y_sample, new_k_prompt, new_v_prompt, new_conv_prompt,
            new_k_sample, new_v_sample, new_conv_sample)
```

```python
import os
from contextlib import ExitStack
import numpy as np
import concourse.bass as bass
import concourse.mybir as mybir
from concourse.bass_utils import run_bass_kernel_spmd

F32 = mybir.dt.float32
BF16 = mybir.dt.bfloat16
AF = mybir.ActivationFunctionType
ALU = mybir.AluOpType

D = 1024
DFF = 2816
DEPTH = 4
SEQ = 4096
TB = 1024
NBLK = SEQ // TB
NS = 16
NCOL = TB + NS
INC = 5120
EPS = 1e-6
NEG = -30000.0
NWSLOT = 8
NT = 6

DBG_NBLK = int(os.environ.get("K_NBLK", NBLK))
DBG_NLAY = int(os.environ.get("K_NLAY", DEPTH))
MSTAGE = int(os.environ.get("K_MSTAGE", 99))
VST = int(os.environ.get("K_VST", 99))


class Tracker:
    def __init__(self, nc, es):
        self.nc = nc
        self.es = es
        self.engs = ["pe", "act", "dve", "pool", "sp"]
        self.sem = {e: es.enter_context(nc.semaphore("s_" + e)) for e in self.engs}
        self.cnt = {e: 0 for e in self.engs}
        self.lists = {e: [] for e in self.engs}
        self.clock = {e: {} for e in self.engs}
        self.lastw = {}
        self.reads = {}
        self.dsem = {}
        self.dcnt = {}
        self.ninstr = 0

    def _dma_sem(self, key):
        if key not in self.dsem:
            self.dsem[key] = self.es.enter_context(self.nc.semaphore("d_%d" % len(self.dsem)))
            self.dcnt[key] = 0
        return key

    def _semobj(self, sid):
        return self.sem[sid] if sid in self.sem else self.dsem[sid]

    def _deps(self, eng, reads, writes, pe_acc=False):
        need = {}

        def add(ev):
            if ev is None:
                return
            s, v = ev
            if need.get(s, 0) < v:
                need[s] = v
        for k in reads:
            add(self.lastw.get(k))
        for k in writes:
            add(self.lastw.get(k))
            for ev in self.reads.get(k, {}).items():
                add(ev)
        waits = []
        ck = self.clock[eng]
        for s, v in need.items():
            if s == "pe" and eng == "pe":
                continue
            if ck.get(s, 0) >= v:
                continue
            ck[s] = v
            waits.append((s, v))
        return waits

    def _commit(self, ev, reads, writes):
        s, v = ev
        for k in reads:
            r = self.reads.setdefault(k, {})
            if r.get(s, 0) < v:
                r[s] = v
        for k in writes:
            self.lastw[k] = ev
            self.reads[k] = {}

    @staticmethod
    def _psum_excl(reads, writes):
        pr = [k for k in reads if isinstance(k, tuple) and k[0] == "ps"]
        if not pr:
            return list(reads), list(writes)
        return [k for k in reads if k not in pr], list(writes) + [k for k in pr if k not in writes]

    def op(self, eng, fn, reads=(), writes=()):
        reads, writes = self._psum_excl(reads, writes)
        waits = self._deps(eng, reads, writes)
        self.cnt[eng] += 1
        ev = (eng, self.cnt[eng])
        self.lists[eng].append((waits, fn, eng, 1))
        self._commit(ev, reads, writes)
        self.ninstr += 1

    def dma(self, q, fn, reads=(), writes=(), semkey=None):
        key = self._dma_sem(semkey)
        waits = self._deps(q, reads, writes)
        prev = self.dcnt[key]
        if prev and self.clock[q].get(key, 0) < prev:
            self.clock[q][key] = prev
            waits.append((key, prev))
        self.dcnt[key] += 16
        ev = (key, self.dcnt[key])
        self.lists[q].append((waits, fn, key, 16))
        self._commit(ev, reads, writes)
        self.ninstr += 1

    def transfer(self, old_keys, new_keys):
        evs = {}
        for k in old_keys:
            ev = self.lastw.get(k)
            if ev is not None and evs.get(ev[0], 0) < ev[1]:
                evs[ev[0]] = ev[1]
            for s_, v in self.reads.get(k, {}).items():
                if evs.get(s_, 0) < v:
                    evs[s_] = v
        for k in new_keys:
            r = self.reads.setdefault(k, {})
            for s_, v in evs.items():
                if r.get(s_, 0) < v:
                    r[s_] = v

    def final_wait(self, q):
        waits = []
        for key, v in self.dcnt.items():
            if v and self.clock[q].get(key, 0) < v:
                waits.append((key, v))
        for e in self.engs:
            if e != q and self.cnt[e]:
                waits.append((e, self.cnt[e]))
        self.lists[q].append((waits, None, None, 0))

    def replay(self, eng, e):
        for waits, fn, sid, inc in self.lists[eng]:
            for s, v in waits:
                e.wait_ge(self._semobj(s), v)
            if fn is not None:
                ins = fn(e)
                ins.then_inc(self._semobj(sid), inc)


def build_program():
    nc = bass.Bass("TRN2", target_bir_lowering=False)
    es = ExitStack()
    with es:
        T = Tracker(nc, es)

        def dram_in(name, shape):
            return nc.dram_tensor(name, list(shape), F32, kind="ExternalInput")

        def dram_out(name, shape):
            return nc.dram_tensor(name, list(shape), F32, kind="ExternalOutput")

        xp = dram_in("xp", [SEQ, D]).ap()
        xs = dram_in("xs", [NS, D]).ap()
        ck = dram_in("ck", [DEPTH, 512, 512]).ap()
        cv = dram_in("cv", [DEPTH, 512, 512]).ap()
        sc = dram_in("sc", [DEPTH, 2, 512]).ap()
        n1 = dram_in("norm_ffn1", [DEPTH, D]).ap()
        w1gu = dram_in("w_ffn1_gu", [DEPTH, D, 2 * DFF]).ap()
        w1d = dram_in("w_ffn1_down", [DEPTH, DFF, D]).ap()
        nm = dram_in("norm_mix", [DEPTH, D]).ap()
        win = dram_in("w_in", [DEPTH, D, INC]).ap()
        bgate = dram_in("b_gate", [DEPTH, 2 * D]).ap()
        wconv = dram_in("w_conv", [DEPTH, 3, 512]).ap()
        relb_t = dram_in("rel_bias", [DEPTH, 8, 257])
        relb = relb_t.ap()
        wco = dram_in("w_conv_out", [DEPTH, 512, D]).ap()
        wao = dram_in("w_attn_out", [DEPTH, 512, D]).ap()
        wo = dram_in("w_o", [DEPTH, D, D]).ap()
        n2 = dram_in("norm_ffn2", [DEPTH, D]).ap()
        w2gu = dram_in("w_ffn2_gu", [DEPTH, D, 2 * DFF]).ap()
        w2d = dram_in("w_ffn2_down", [DEPTH, DFF, D]).ap()
        nf = dram_in("norm_final", [D]).ap()

        yp = dram_out("yp", [SEQ, D]).ap()
        ys = dram_out("ys", [NS, D]).ap()
        okp = dram_out("okp", [DEPTH, 512, 512]).ap()
        ovp = dram_out("ovp", [DEPTH, 512, 512]).ap()
        ocp = dram_out("ocp", [DEPTH, 2, 512]).ap()
        oks = dram_out("oks", [DEPTH, NS, 512]).ap()
        ovs = dram_out("ovs", [DEPTH, NS, 512]).ap()
        ocs = dram_out("ocs", [DEPTH, 2, 512]).ap()
        ext_t = nc.dram_tensor("ext_scratch", [DEPTH, 8, 384], F32, kind="Internal")
        ext = ext_t.ap()

        def sb(name, shape, dt=F32):
            return es.enter_context(nc.sbuf_tensor(name, list(shape), dt))

        xT = sb("xT", [128, 8, NCOL])
        hT = sb("hT", [128, 8, NCOL], BF16)
        wbig = sb("wbig", [128, NWSLOT * 2048], BF16)
        histK = [sb("hK%d" % l, [128, 4, 512], BF16) for l in range(DEPTH)]
        histV = [sb("hV%d" % l, [128, 4, 8, 65], BF16) for l in range(DEPTH)]
        histU = [sb("hU%d" % l, [128, 4, 2]) for l in range(DEPTH)]
        gains = sb("gains", [128, 104])
        bgT = sb("bgT", [128, 64])
        wcT = sb("wcT", [128, 48])
        ident = sb("ident", [128, 128])
        antiI = sb("antiI", [128, 128])
        ones_f = sb("ones_f", [128, 128])
        ones_b = sb("ones_b", [128, 128], BF16)
        cstage = sb("cstage", [128, 2, 128])
        Bt = sb("Bt", [128, 2, 8, 128])
        cvec = sb("cvec", [128, 8])
        xstage = [sb("xst%d" % i, [128, 512]) for i in range(4)]
        ostage = xstage
        tmpf = [sb("tmpf%d" % i, [128, 512]) for i in range(NT)]
        aT = [sb("aT%d" % i, [128, 4, NCOL], BF16) for i in range(2)]
        qT = aT[0]
        yaT = aT[1]
        arenaC = sb("arenaC", [128, 4 * NCOL])
        mT = arenaC[:].bitcast(BF16).rearrange("p (o t) -> p o t", o=8)
        uT = arenaC[:, 0:4 * 514].rearrange("p (c t) -> p c t", c=4)
        yconv = arenaC[:, 4 * 514:4 * 514 + 2048].rearrange("p (c t) -> p c t", c=4)
        KTb = sb("KTb", [128, 4, NCOL], BF16)
        KT = KTb[:, :, 0:TB]
        ycT = KTb
        V65 = sb("V65", [128, 8, 8, 65], BF16)
        KTs = sb("KTs", [128, 4, 512 + NS], BF16)
        V65s = sb("V65s", [128, 5, 8, 65], BF16)
        Pfar = [sb("Pfar%d" % i, [128, 384], BF16) for i in range(4)]
        Pnear = [sb("Pnear%d" % i, [128, 256], BF16) for i in range(2)]
        yatok = tmpf
        kvst = tmpf
        rden = sb("rden", [128, 8])
        ust = sb("ust", [128, 4, 2])

        psum = [es.enter_context(nc.psum_tensor("ps%d" % i, [128, 512], F32)) for i in range(8)]
        st = {"ps": 0, "w": 0, "tmp": 0, "xst": 0, "pf": 0, "pn": 0, "kv": 0, "ya": 0}

        pinned = set()

        def next_ps():
            i = st["ps"]
            while i in pinned:
                i = (i + 1) % 8
            st["ps"] = (i + 1) % 8
            return i

        def rot(name, n):
            i = st[name]
            st[name] = (i + 1) % n
            return i

        def mm(out, lhsT, rhs, start, stop, reads, writes):
            T.op("pe", lambda e: e.matmul(out, lhsT=lhsT, rhs=rhs, start=start, stop=stop),
                 reads=reads, writes=writes)

        def act(out, in_, func, reads, writes, **kw):
            T.op("act", lambda e: e.activation(out=out, in_=in_, func=func, **kw), reads=reads, writes=writes)

        def dve(fn, reads, writes):
            T.op("dve", fn, reads=reads, writes=writes)

        def load_w(src_ap, shape3, name, slot=None):
            a, b_ = shape3
            ns = (a * b_ + 2047) // 2048
            if slot is not None:
                i = slot
            else:
                i = st["w"]
                if ns == 2 and i % 2 == 1:
                    i = (i + 1) % NWSLOT
                st["w"] = (i + ns) % NWSLOT
            keys = [("w", i + j) for j in range(ns)]
            dst = wbig[:, i * 2048: i * 2048 + a * b_].rearrange("p (a b) -> p a b", a=a)
            T.dma("pool", lambda e: e.dma_start(out=dst, in_=src_ap), reads=[], writes=keys,
                  semkey=("w", i))
            return keys, dst

        T.op("pool", lambda e: e.memset(ones_f[:], 1.0), writes=["ones_f"])
        T.op("pool", lambda e: e.memset(ones_b[:], 1.0), writes=["ones_b"])
        T.op("pool", lambda e: e.affine_select(out=ident[:], in_=ones_f[:], pattern=[[-1, 128]],
                                               compare_op=ALU.is_equal, fill=0.0, base=0,
                                               channel_multiplier=1),
             reads=["ones_f"], writes=["ident"])
        T.op("pool", lambda e: e.affine_select(out=antiI[:], in_=ones_f[:], pattern=[[1, 128]],
                                               compare_op=ALU.is_equal, fill=0.0, base=-127,
                                               channel_multiplier=1),
             reads=["ones_f"], writes=["antiI"])
        for i in range(4):
            T.op("pool", (lambda i: lambda e: e.memset(Pfar[i][:], 0.0))(i), writes=[("pf", i)])
        for l in range(DEPTH):
            T.op("pool", (lambda l: lambda e: e.memset(histV[l][:], 1.0))(l), writes=[("hV", l)])
            T.op("pool", (lambda l: lambda e: e.memset(histU[l][:], 0.0))(l), writes=[("hU", l)])
        T.op("pool", lambda e: e.memset(V65[:], 1.0), writes=[("V65", t) for t in range(8)])
        T.op("pool", lambda e: e.memset(V65s[:], 1.0), writes=["V65s"])

        T.op("pool", lambda e: e.memset(cstage[:], 0.0), writes=["cstage"])
        T.dma("sp", lambda e: e.dma_start(out=cstage[0:32, 0, :], in_=n1.rearrange("l (k p) -> (l k) p", p=128)),
              writes=["cstage"], reads=["cstage"], semkey="c0")
        T.dma("sp", lambda e: e.dma_start(out=cstage[32:64, 0, :], in_=nm.rearrange("l (k p) -> (l k) p", p=128)),
              writes=["cstage"], reads=["cstage"], semkey="c1")
        T.dma("sp", lambda e: e.dma_start(out=cstage[64:96, 0, :], in_=n2.rearrange("l (k p) -> (l k) p", p=128)),
              writes=["cstage"], reads=["cstage"], semkey="c2")
        T.dma("sp", lambda e: e.dma_start(out=cstage[96:104, 0, :], in_=nf.rearrange("(k p) -> k p", p=128)),
              writes=["cstage"], reads=["cstage"], semkey="c3")
        T.dma("sp", lambda e: e.dma_start(out=cstage[0:64, 1, :], in_=bgate.rearrange("l (k p) -> (l k) p", p=128)),
              writes=["cstage"], reads=["cstage"], semkey="c4")
        T.dma("sp", lambda e: e.dma_start(out=cstage[64:112, 1, :],
                                          in_=wconv.rearrange("l j (k p) -> (l j k) p", p=128)),
              writes=["cstage"], reads=["cstage"], semkey="c5")
        p0 = next_ps()
        T.op("pe", lambda e: e.transpose(psum[p0][:, 0:128], cstage[:, 0, :], ident[:]),
             reads=["cstage", "ident"], writes=[("ps", p0)])
        T.op("pe", lambda e: e.transpose(psum[p0][:, 128:256], cstage[:, 1, :], ident[:]),
             reads=["cstage", "ident"], writes=[("ps", p0)])
        dve(lambda e: e.tensor_copy(out=gains[:], in_=psum[p0][:, 0:104]), [("ps", p0)], ["gains"])
        dve(lambda e: e.tensor_copy(out=bgT[:], in_=psum[p0][:, 128:192]), [("ps", p0)], ["bgT"])
        dve(lambda e: e.tensor_copy(out=wcT[:], in_=psum[p0][:, 192:240]), [("ps", p0)], ["wcT"])

        T.dma("sp", lambda e: e.dma_start(out=ext[:, :, 0:257], in_=relb), writes=["ext"], semkey="e0")
        rbs = sb("rbs", [32, 1])
        rbb = sb("rbb", [32, 127])
        T.dma("sp", lambda e: e.dma_start(out=rbs[:], in_=bass.AP(relb_t, 256, [[257, 32], [1, 1]]),
                                          allow_slow_non_contiguous=True),
              writes=["rbs"], semkey="e1")
        dve(lambda e: e.tensor_copy(out=rbb[:], in_=rbs[:, 0:1].to_broadcast([32, 127])), ["rbs"], ["rbb"])
        T.dma("sp", lambda e: e.dma_start(out=ext.rearrange("l h i -> (l h) i")[:, 257:384], in_=rbb[:]),
              writes=["ext"], reads=["ext", "rbb"], semkey="e2")

        def tiles_of(b):
            t = [(0, 512), (512, 512)]
            if b == 0:
                t.append((TB, NS))
            return t

        def load_x_block(b):
            tiles = []
            for tt in range(8):
                tiles.append((xp[b * TB + tt * 128: b * TB + (tt + 1) * 128, :], 128, tt * 128))
            if b == 0:
                tiles.append((xs, NS, TB))
            for src, n, col in tiles:
                for half in range(2):
                    si = rot("xst", 4)
                    T.dma("sp", (lambda si, src, n, half: lambda e: e.dma_start(
                        out=xstage[si][0:n, :], in_=src[:, half * 512:(half + 1) * 512]))(si, src, n, half),
                        writes=[("xst", si)], semkey=("xst", si))
                    pi = next_ps()
                    for j in range(4):
                        T.op("pe", (lambda pi, j, si, n: lambda e: e.transpose(
                            psum[pi][:, j * 128: j * 128 + n], xstage[si][0:n, j * 128:(j + 1) * 128],
                            ident[0:n, 0:n]))(pi, j, si, n),
                            reads=[("xst", si), "ident"], writes=[("ps", pi)])
                    srcp = psum[pi][:].rearrange("p (j t) -> p j t", j=4)[:, :, 0:n]
                    dstx = xT[:, half * 4: half * 4 + 4, col: col + n]
                    wk = [("xT", kk, col // 512) for kk in range(half * 4, half * 4 + 4)]
                    if half == 0:
                        dve((lambda dstx, srcp: lambda e: e.tensor_copy(out=dstx, in_=srcp))(dstx, srcp),
                            [("ps", pi)] + wk, wk)
                    else:
                        act(dstx, srcp, AF.Copy, [("ps", pi)] + wk, wk)

        def rms_stats(c0, w, ti):
            pi = next_ps()
            for k in range(8):
                qi = rot("tmp", NT)
                sqv = tmpf[qi][:].bitcast(BF16)[:, 0:w]
                act(sqv, xT[:, k, c0:c0 + w], AF.Square, [("xT", k, ti), ("tmp", qi)], [("tmp", qi)])
                mm(psum[pi][:, 0:w], ones_b[:], sqv, k == 0, k == 7, [("tmp", qi), "ones_b"], [("ps", pi)])
            si = rot("tmp", NT)
            act(tmpf[si][:, 0:w], psum[pi][:, 0:w], AF.Ln, [("ps", pi), ("tmp", si), "epsv"], [("tmp", si)],
                scale=1.0 / D, bias=epsv[:, 0:1])
            ri = rot("tmp", NT)
            act(tmpf[ri][:, 0:w], tmpf[si][:, 0:w], AF.Exp, [("tmp", si), ("tmp", ri)], [("tmp", ri)], scale=-0.5)
            return ri

        class TailStats:
            def __init__(self, b, apply=None):
                act(dummy[0:1, 0:1], epsv[0:1, 0:1], AF.Ln, ["epsv", "dummy"], ["dummy"])
                self.apply = apply
                self.later = []
                self.seen = {}
                self.out = {}
                self.tiles = tiles_of(b)
                self.bank = {}
                self.pend = []
                self.cnt = {}
                for (c0, w) in self.tiles:
                    ti = c0 // 512
                    self.bank[ti] = next_ps()
                    pinned.add(self.bank[ti])
                    self.cnt[ti] = 0

            def _flush(self, keep):
                while len(self.pend) > keep:
                    (ti, w, qi) = self.pend.pop(0)
                    n = self.cnt[ti]
                    self.cnt[ti] += 1
                    pi = self.bank[ti]
                    sqv = tmpf[qi][:].bitcast(BF16)[:, 0:w]
                    mm(psum[pi][:, 0:w], ones_b[:], sqv, n == 0, n == 7, [("tmp", qi), "ones_b"], [("ps", pi)])

            def chunk_done(self, k, c0, w):
                ti = c0 // 512
                qi = rot("tmp", NT)
                sqv = tmpf[qi][:].bitcast(BF16)[:, 0:w]
                act(sqv, xT[:, k, c0:c0 + w], AF.Square, [("xT", k, ti), ("tmp", qi)], [("tmp", qi)])
                self.pend.append((ti, w, qi))
                self.seen[ti] = self.seen.get(ti, 0) + 1
                for item in self.later:
                    item[0] -= 1
                for item in [it for it in self.later if it[0] <= 0]:
                    self.later.remove(item)
                    self._flush_tile(item[3])
                    self._complete(item[1], item[2], item[3])
                if self.seen[ti] == 8:
                    self.later.append([3, c0, w, ti])
                else:
                    self._flush(3)

            def _flush_tile(self, ti):
                keep = [p for p in self.pend if p[0] != ti]
                mine = [p for p in self.pend if p[0] == ti]
                self.pend = mine
                self._flush(0)
                self.pend = keep

            def _complete(self, c0, w, ti):
                pi = self.bank[ti]
                si = rot("tmp", NT)
                act(tmpf[si][:, 0:w], psum[pi][:, 0:w], AF.Ln, [("ps", pi), ("tmp", si), "epsv"], [("tmp", si)],
                    scale=1.0 / D, bias=epsv[:, 0:1])
                ri = rot("tmp", NT)
                act(tmpf[ri][:, 0:w], tmpf[si][:, 0:w], AF.Exp, [("tmp", si), ("tmp", ri)], [("tmp", ri)], scale=-0.5)
                pinned.discard(pi)
                if self.apply is not None:
                    self.apply(c0, w, ti, ri)
                    self.out[ti] = None
                else:
                    self.out[ti] = ri

            def finish(self):
                assert all(self.seen.get(c0 // 512, 0) == 8 for (c0, w) in self.tiles)
                for item in list(self.later):
                    self._flush_tile(item[3])
                    self._complete(item[1], item[2], item[3])
                self.later = []
                return self.out

        def emit_hT(gidx):
            def f(c0, w, ti, ri):
                for k in range(8):
                    dve((lambda k, c0, w, ri: lambda e: e.scalar_tensor_tensor(
                        out=hT[:, k, c0:c0 + w], in0=xT[:, k, c0:c0 + w],
                        scalar=gains[:, gidx + k: gidx + k + 1], in1=tmpf[ri][:, 0:w],
                        op0=ALU.mult, op1=ALU.mult))(k, c0, w, ri),
                        [("xT", k, ti), ("tmp", ri), "gains"], [("hT", k, ti)])
            return f

        def emit_final_scale(c0, w, ti, ri):
            for k in range(8):
                dve((lambda k, c0, w, ri: lambda e: e.scalar_tensor_tensor(
                    out=xT[:, k, c0:c0 + w], in0=xT[:, k, c0:c0 + w],
                    scalar=gains[:, 96 + k: 97 + k], in1=tmpf[ri][:, 0:w],
                    op0=ALU.mult, op1=ALU.mult))(k, c0, w, ri),
                    [("xT", k, ti), ("tmp", ri), "gains"], [("xT", k, ti)])

        def rmsnorm_to_hT(b, gidx, pre=None):
            for (c0, w) in tiles_of(b):
                ti = c0 // 512
                if pre is not None and pre[ti] is None:
                    continue
                ri = pre[ti] if pre is not None else rms_stats(c0, w, ti)
                for k in range(8):
                    dve((lambda k, c0, w, ri: lambda e: e.scalar_tensor_tensor(
                        out=hT[:, k, c0:c0 + w], in0=xT[:, k, c0:c0 + w],
                        scalar=gains[:, gidx + k: gidx + k + 1], in1=tmpf[ri][:, 0:w],
                        op0=ALU.mult, op1=ALU.mult))(k, c0, w, ri),
                        [("xT", k, ti), ("tmp", ri), "gains"], [("hT", k, ti)])

        epsv = sb("epsv", [128, 1])
        dummy = sb("lnwarm", [128, 1])
        T.op("pool", lambda e: e.memset(epsv[:], EPS), writes=["epsv"])

        def ffn(b, l, wgu, wd, gidx, pre=None, tail=False, mid_hook=None, tail_apply=None):
            T.transfer([("qT", m, ti) for m in range(4) for ti in range(3)],
                       [("aT", 0, fc, ti) for fc in range(4) for ti in range(3)])
            T.transfer([("yaT", m, ti) for m in range(4) for ti in range(3)],
                       [("aT", 1, fc, ti) for fc in range(4) for ti in range(3)])
            rmsnorm_to_hT(b, gidx, pre)
            tiles = tiles_of(b)
            tstats = [None]
            NSUB = 11
            sgs = [(0,), (1, 2), (3, 4), (5, 6), (7, 8), (9, 10)]
            wguv = wgu[l].rearrange("(k p) n -> p k n", p=128)
            wdv = wd[l]
            loaded = {}

            def gu_units(si):
                ab = si % 2
                units = []
                for sub_i, n in enumerate(sgs[si]):
                    f0 = n * 256
                    holder = {}

                    def loads(n=n, f0=f0, holder=holder):
                        holder["g"] = load_w(wguv[:, :, f0:f0 + 256], (8, 256), "wg", slot=(2 * n) % 4)
                        holder["u"] = load_w(wguv[:, :, DFF + f0: DFF + f0 + 256], (8, 256), "wu", slot=(2 * n + 1) % 4)
                        loaded[n] = load_w(wdv[f0:f0 + 256, :].rearrange("(c p) n -> p c n", p=128), (2, D), "wd",
                                           slot=4 + n % 4)
                    first = True
                    for (f, (c0, w)) in ([(f, t) for f in range(2) for t in tiles[:2]] +
                                         [(f, t) for f in range(2) for t in tiles[2:]]):
                        fc = sub_i * 2 + f

                        def unit(f=f, c0=c0, w=w, fc=fc, holder=holder, pre=(loads if first else None)):
                            if pre is not None:
                                pre()
                            ig, wg = holder["g"]
                            iu, wu = holder["u"]
                            ti = c0 // 512
                            pg = next_ps()
                            for k in range(8):
                                mm(psum[pg][:, 0:w], wg[:, k, f * 128:(f + 1) * 128], hT[:, k, c0:c0 + w],
                                   k == 0, k == 7, ig + [("hT", k, ti)], [("ps", pg)])
                            pu = next_ps()
                            for k in range(8):
                                mm(psum[pu][:, 0:w], wu[:, k, f * 128:(f + 1) * 128], hT[:, k, c0:c0 + w],
                                   k == 0, k == 7, iu + [("hT", k, ti)], [("ps", pu)])
                            tmi = rot("tmp", NT)
                            act(tmpf[tmi][:, 0:w], psum[pg][:, 0:w], AF.Silu, [("ps", pg)], [("tmp", tmi)])
                            dve((lambda ab, fc, c0, w, pu, tmi: lambda e: e.tensor_tensor(
                                out=aT[ab][:, fc, c0:c0 + w], in0=psum[pu][:, 0:w], in1=tmpf[tmi][:, 0:w],
                                op=ALU.mult))(ab, fc, c0, w, pu, tmi),
                                [("ps", pu), ("tmp", tmi)], [("aT", ab, fc, ti)])
                        units.append(unit)
                        first = False
                return units

            def dn_units(si):
                ab = si % 2
                subs = sgs[si]
                nfc = 2 * len(subs)
                units = []
                if tstats[0] is not None:
                    order = [(o, t) for t in tiles for o in range(8)]
                else:
                    order = ([(o, t) for o in range(8) for t in tiles[:2]] +
                             [(o, t) for o in range(8) for t in tiles[2:]])
                for (o, (c0, w)) in order:
                    def unit(o=o, c0=c0, w=w):
                        ti = c0 // 512
                        pd = next_ps()
                        for fc in range(nfc):
                            idn, wdn = loaded[subs[fc // 2]]
                            mm(psum[pd][:, 0:w], wdn[:, fc % 2, o * 128:(o + 1) * 128], aT[ab][:, fc, c0:c0 + w],
                               fc == 0, fc == nfc - 1, idn + [("aT", ab, fc, ti)], [("ps", pd)])
                        dve((lambda o, c0, w, pd: lambda e: e.scalar_tensor_tensor(
                            out=xT[:, o, c0:c0 + w], in0=psum[pd][:, 0:w], scalar=0.5,
                            in1=xT[:, o, c0:c0 + w], op0=ALU.mult, op1=ALU.add))(o, c0, w, pd),
                            [("ps", pd), ("xT", o, ti)], [("xT", o, ti)])
                        if tstats[0] is not None:
                            tstats[0].chunk_done(o, c0, w)
                    units.append(unit)
                return units

            for u in gu_units(0):
                u()
            for si in range(len(sgs)):
                A = gu_units(si + 1) if si + 1 < len(sgs) else []
                if tail and si == len(sgs) - 1:
                    tstats[0] = TailStats(b, tail_apply)
                Bn = dn_units(si)
                if A:
                    per = -(-len(Bn) // len(A))
                    bi = 0
                    for a in A:
                        a()
                        for _ in range(per):
                            if bi < len(Bn):
                                Bn[bi]()
                                bi += 1
                    while bi < len(Bn):
                        Bn[bi]()
                        bi += 1
                else:
                    for u in Bn:
                        u()
                if si == 1 and mid_hook is not None:
                    mid_hook()
            st["w"] = 0
            return tstats[0].finish() if tstats[0] is not None else None

        def proj_fm(wunit, widx, ncols_chunks, k_n, rhs_buf, rhs_key, tiles, consume):
            for (m, (c0, w)) in ([(m, t) for m in range(ncols_chunks) for t in tiles[:2]] +
                                 [(m, t) for m in range(ncols_chunks) for t in tiles[2:]]):
                if True:
                    ti = c0 // 512
                    pi = next_ps()
                    for k in range(k_n):
                        mm(psum[pi][:, 0:w], wunit[:, k, m * 128:(m + 1) * 128], rhs_buf[:, k, c0:c0 + w],
                           k == 0, k == k_n - 1, widx + [(rhs_key, k, ti)], [("ps", pi)])
                    consume(m, c0, w, ti, pi)

        def build_bias(l):
            T.dma("sp", lambda e: e.dma_start(
                out=cvec[:], in_=bass.AP(relb_t, l * 8 * 257 + 256, [[0, 128], [257, 8]]),
                allow_slow_non_contiguous=True),
                writes=["cvec"], semkey="cvec")
            Bf = Bt[:].rearrange("p a h q -> p (a h q)")
            for j in range(4):
                ty, hh = j // 2, (j % 2) * 4
                tmi = rot("tmp", NT)
                T.dma("sp", (lambda ty, hh, tmi: lambda e: e.dma_start(
                    out=tmpf[tmi][:].rearrange("p (h q) -> p h q", h=4),
                    in_=bass.AP(ext_t, l * 8 * 384 + hh * 384 + 1 + 128 * ty, [[1, 128], [384, 4], [1, 128]])))(ty, hh, tmi),
                    reads=["ext"], writes=[("tmp", tmi)], semkey=("ab", tmi))
                pi = next_ps()
                T.op("pe", (lambda pi, tmi: lambda e: e.matmul(psum[pi][:], lhsT=antiI[:], rhs=tmpf[tmi][:],
                                                              start=True, stop=True))(pi, tmi),
                     reads=[("tmp", tmi), "antiI"], writes=[("ps", pi)])
                act(Bf[:, j * 512:(j + 1) * 512], psum[pi][:], AF.Copy, [("ps", pi)], ["Bt"])
            dve(lambda e: e.memset(Bt[64:128, 0, :, 0:64], NEG), ["Bt"], ["Bt"])

        bias_prebuilt = [False]

        def mixer(b, l, pre=None, tail=False, tail_apply=None):
            tiles = tiles_of(b)
            ptiles = tiles[:2]
            last = (b == DBG_NBLK - 1)
            aT0k = [("aT", 0, fc, ti) for fc in range(4) for ti in range(3)]
            aT1k = [("aT", 1, fc, ti) for fc in range(4) for ti in range(3)]
            qTk = [("qT", m, ti) for m in range(4) for ti in range(3)]
            yaTk = [("yaT", m, ti) for m in range(4) for ti in range(3)]
            mTk = [("mT", m, ti) for m in range(8) for ti in range(3)]
            T.transfer(aT0k, qTk)
            T.transfer(aT1k, yaTk)
            T.transfer(mTk, ["uT", "yconv"])
            T.transfer([("ycT", c, ti) for c in range(4) for ti in range(3)], [("KT", t) for t in range(8)])
            rmsnorm_to_hT(b, 32 + l * 8, pre)
            if not bias_prebuilt[0]:
                build_bias(l)
            bias_prebuilt[0] = False
            winv = win[l].rearrange("(k p) n -> p k n", p=128)

            def tok_major_rows(wun, iun, hq, outp, outs_, is_v):
                if is_v:
                    tts = range(8)
                elif last:
                    tts = range(4, 8)
                else:
                    tts = []
                for tt in tts:
                    pi = next_ps()
                    for k in range(8):
                        mm(psum[pi][:, 0:256], hT[:, k, tt * 128:(tt + 1) * 128], wun[:, k, :], k == 0, k == 7,
                           iun + [("hT", k, tt // 4)], [("ps", pi)])
                    if is_v:
                        act(V65[:, tt, hq * 4:(hq + 1) * 4, 0:64], psum[pi][:, 0:256].rearrange("p (h d) -> p h d", h=4),
                            AF.Copy, [("ps", pi), ("V65", tt)], [("V65", tt)])
                    if last and tt >= 4:
                        ki = rot("tmp", NT)
                        dve((lambda ki, pi: lambda e: e.tensor_copy(out=kvst[ki][:, 0:256], in_=psum[pi][:, 0:256]))(ki, pi),
                            [("ps", pi), ("tmp", ki)], [("tmp", ki)])
                        T.dma("sp", (lambda ki, tt: lambda e: e.dma_start(
                            out=outp[l, (tt - 4) * 128:(tt - 3) * 128, hq * 256:(hq + 1) * 256], in_=kvst[ki][:, 0:256]))(ki, tt),
                            reads=[("tmp", ki)], semkey=("tmpd", ki))
                if b == 0:
                    pi = next_ps()
                    for k in range(8):
                        mm(psum[pi][0:NS, 0:256], hT[:, k, TB:TB + NS], wun[:, k, :], k == 0, k == 7,
                           iun + [("hT", k, 2)], [("ps", pi)])
                    if is_v:
                        act(V65s[0:NS, 4, hq * 4:(hq + 1) * 4, 0:64], psum[pi][0:NS, 0:256].rearrange("p (h d) -> p h d", h=4),
                            AF.Copy, [("ps", pi), "V65s"], ["V65s"])
                    ki = rot("tmp", NT)
                    dve((lambda ki, pi: lambda e: e.tensor_copy(out=kvst[ki][0:NS, 0:256], in_=psum[pi][0:NS, 0:256]))(ki, pi),
                        [("ps", pi), ("tmp", ki)], [("tmp", ki)])
                    T.dma("sp", (lambda ki: lambda e: e.dma_start(
                        out=outs_[l][:, hq * 256:(hq + 1) * 256], in_=kvst[ki][0:NS, 0:256]))(ki),
                        reads=[("tmp", ki)], semkey=("tmpd", ki))

            for hq in range(2):
                iq, wq = load_w(winv[:, :, 1536 + hq * 256:1536 + (hq + 1) * 256], (8, 256), "q")

                def cons_q(m, c0, w, ti, pi, hq=hq):
                    act(qT[:, hq * 2 + m, c0:c0 + w], psum[pi][:, 0:w], AF.Copy, [("ps", pi)], [("qT", hq * 2 + m, ti)])
                proj_fm(wq, iq, 2, 8, hT, "hT", tiles, cons_q)
            for hq in range(2):
                ik, wk = load_w(winv[:, :, 2048 + hq * 256:2048 + (hq + 1) * 256], (8, 256), "k")

                def cons_k(m, c0, w, ti, pi, hq=hq):
                    mm_ = hq * 2 + m
                    if c0 == TB:
                        act(KTs[:, mm_, 512:512 + NS], psum[pi][:, 0:w], AF.Copy, [("ps", pi)], ["KTs"])
                    else:
                        dve((lambda mm_, c0, w, pi: lambda e: e.tensor_copy(out=KT[:, mm_, c0:c0 + w], in_=psum[pi][:, 0:w]))(mm_, c0, w, pi),
                            [("ps", pi)] + [("KT", c0 // 128 + j) for j in range(4)],
                            [("KT", c0 // 128 + j) for j in range(4)])
                proj_fm(wk, ik, 2, 8, hT, "hT", tiles, cons_k)
                tok_major_rows(wk, ik, hq, okp, oks, False)
            for hq in range(2):
                iv, wv = load_w(winv[:, :, 2560 + hq * 256:2560 + (hq + 1) * 256], (8, 256), "v")
                tok_major_rows(wv, iv, hq, ovp, ovs, True)
            if b == 0:
                for t4 in range(4):
                    T.dma("pool", (lambda t4: lambda e: e.dma_start(
                        out=V65s[:, t4, :, 0:64],
                        in_=cv[l, t4 * 128:(t4 + 1) * 128, :].rearrange("p (h d) -> p h d", h=8)))(t4),
                        reads=["V65s"], writes=["V65s"], semkey="cvs")
                for t4 in range(4):
                    ki = rot("tmp", NT)
                    T.dma("sp", (lambda ki, t4: lambda e: e.dma_start(out=kvst[ki][:], in_=ck[l, t4 * 128:(t4 + 1) * 128, :]))(ki, t4),
                          writes=[("tmp", ki)], semkey=("tmpd", ki))
                    pi = next_ps()
                    for m in range(4):
                        T.op("pe", (lambda pi, ki, m: lambda e: e.transpose(
                            psum[pi][:, m * 128:(m + 1) * 128], kvst[ki][:, m * 128:(m + 1) * 128], ident[:]))(pi, ki, m),
                            reads=[("tmp", ki), "ident"], writes=[("ps", pi)])
                    dve((lambda t4, pi: lambda e: e.tensor_copy(
                        out=KTs[:, :, t4 * 128:(t4 + 1) * 128], in_=psum[pi][:].rearrange("p (m t) -> p m t", m=4)))(t4, pi),
                        [("ps", pi), "KTs"], ["KTs"])

            if MSTAGE <= 3:
                return
            def key_tile(kt_local):
                if kt_local < 0:
                    t = kt_local + 4
                    return (lambda hp: histK[l][:, hp, t * 128:(t + 1) * 128]), ("hK", l), \
                           (lambda h: histV[l][:, t, h, :]), ("hV", l)
                t = kt_local
                return (lambda hp: KT[:, hp, t * 128:(t + 1) * 128]), ("KT", t), \
                       (lambda h: V65[:, t, h, :]), ("V65", t)

            pstate = {}

            def att_A(j, h):
                pglob = b * 8 + j
                nj = min(5, pglob + 1)
                q0 = j * 128
                qti = q0 // 512
                if h == 0:
                    oacc = [next_ps(), next_ps()]
                    pinned.update(oacc)
                    pstate[j] = oacc
                hp, hb = h // 2, (h % 2) * 64
                qap = qT[hb:hb + 64, hp, q0:q0 + 128]
                nnear = min(2, nj)
                nfar = nj - nnear
                pfi = None
                if nfar > 0:
                    pf = next_ps()
                    for jp in range(2, nj):
                        kf, kkey, vf, vkey = key_tile(j - jp)
                        mm(psum[pf][:, (jp - 2) * 128:(jp - 1) * 128], kf(hp)[hb:hb + 64, :], qap, True, True,
                           [kkey, ("qT", hp, qti)], [("ps", pf)])
                pn = next_ps()
                for jp in range(nnear):
                    kf, kkey, vf, vkey = key_tile(j - jp)
                    mm(psum[pn][:, jp * 128:(jp + 1) * 128], kf(hp)[hb:hb + 64, :], qap, True, True,
                       [kkey, ("qT", hp, qti)], [("ps", pn)])
                tmi = rot("tmp", NT)
                dve((lambda pn, tmi, h, nnear: lambda e: e.scalar_tensor_tensor(
                    out=tmpf[tmi][:, 0:nnear * 128].rearrange("p (a q) -> p a q", a=nnear),
                    in0=psum[pn][:, 0:nnear * 128].rearrange("p (a q) -> p a q", a=nnear),
                    scalar=0.125, in1=Bt[:, 0:nnear, h, :], op0=ALU.mult, op1=ALU.add))(pn, tmi, h, nnear),
                    [("ps", pn), "Bt", ("tmp", tmi)], [("tmp", tmi)])
                if nfar > 0:
                    pfi = rot("pf", 4)
                    if nfar == 3:
                        act(Pfar[pfi][:, 0:384], psum[pf][:, 0:384], AF.Exp, [("ps", pf), ("pf", pfi), "cvec"], [("pf", pfi)],
                            scale=0.125, bias=cvec[:, h:h + 1])
                        dve((lambda pfi: lambda e: e.memset(Pfar[pfi][0:64, 320:384], 0.0))(pfi), [("pf", pfi)], [("pf", pfi)])
                    else:
                        act(Pfar[pfi][:, 0:nfar * 128], psum[pf][:, 0:nfar * 128], AF.Exp,
                            [("ps", pf), ("pf", pfi), "cvec"], [("pf", pfi)], scale=0.125, bias=cvec[:, h:h + 1])
                pni = rot("pn", 2)
                act(Pnear[pni][:, 0:nnear * 128], tmpf[tmi][:, 0:nnear * 128], AF.Exp,
                    [("tmp", tmi), ("pn", pni)], [("pn", pni)])
                return (nj, pni, pfi)

            def att_B(j, h, info):
                nj, pni, pfi = info
                ob = pstate[j][h // 4]
                osl = psum[ob][:, (h % 4) * 65:(h % 4) * 65 + 65]
                order = list(range(2, nj)) + list(range(min(2, nj)))
                for oi_, jp in enumerate(order):
                    kf, kkey, vf, vkey = key_tile(j - jp)
                    if jp < 2:
                        lhs = Pnear[pni][:, jp * 128:(jp + 1) * 128]
                        rk = ("pn", pni)
                    else:
                        lhs = Pfar[pfi][:, (jp - 2) * 128:(jp - 1) * 128]
                        rk = ("pf", pfi)
                    mm(osl, lhs, vf(h), oi_ == 0, oi_ == nj - 1, [rk, vkey], [("ps", ob)])

            def att_finish(j):
                oacc = pstate[j]
                q0 = j * 128
                qti = q0 // 512
                yi = rot("tmp", NT)
                for half in range(2):
                    ob = oacc[half]
                    o3 = psum[ob][:, 0:260].rearrange("p (h e) -> p h e", h=4)
                    dve((lambda o3, half: lambda e: e.reciprocal(out=rden[:, half * 4:half * 4 + 4], in_=o3[:, :, 64]))(o3, half),
                        [("ps", ob), "rden"], ["rden"])
                    dve((lambda o3, half, yi: lambda e: e.tensor_tensor(
                        out=yatok[yi][:, half * 256:(half + 1) * 256].rearrange("p (h d) -> p h d", h=4),
                        in0=o3[:, :, 0:64],
                        in1=rden[:, half * 4:half * 4 + 4].unsqueeze(2).to_broadcast([128, 4, 64]),
                        op=ALU.mult))(o3, half, yi),
                        [("ps", ob), "rden", ("tmp", yi)], [("tmp", yi)])
                pinned.difference_update(oacc)

                def tr():
                    pt = next_ps()
                    for c in range(4):
                        T.op("pe", (lambda pt, c, yi: lambda e: e.transpose(
                            psum[pt][:, c * 128:(c + 1) * 128], yatok[yi][:, c * 128:(c + 1) * 128], ident[:]))(pt, c, yi),
                            reads=[("tmp", yi), "ident"], writes=[("ps", pt)])
                    act(yaT[:, :, q0:q0 + 128], psum[pt][:].rearrange("p (c q) -> p c q", c=4), AF.Copy,
                        [("ps", pt)] + [("yaT", c, qti) for c in range(4)], [("yaT", c, qti) for c in range(4)])
                return tr

            units = [(j, h) for j in range(8) for h in range(8)]
            pending = []
            infoA = att_A(*units[0])
            for ui, (j, h) in enumerate(units):
                nxt = att_A(*units[ui + 1]) if ui + 1 < len(units) else None
                for item in pending:
                    item[0] -= 1
                for item in [it for it in pending if it[0] <= 0]:
                    item[1]()
                    pending.remove(item)
                att_B(j, h, infoA)
                if h == 7:
                    pending.append([3, att_finish(j)])
                infoA = nxt
            for item in pending:
                item[1]()

            if MSTAGE <= 4:
                return
            if b == 0:
                oacc = [next_ps(), next_ps()]
                pinned.update(oacc)
                def s_A(h):
                    hp, hb = h // 2, (h % 2) * 64
                    qap = qT[hb:hb + 64, hp, TB:TB + NS]
                    pf = next_ps()
                    for r in range(4):
                        mm(psum[pf][:, r * NS:(r + 1) * NS], KTs[hb:hb + 64, hp, r * 128:(r + 1) * 128], qap, True, True,
                           ["KTs", ("qT", hp, 2)], [("ps", pf)])
                    mm(psum[pf][0:NS, 64:64 + NS], KTs[hb:hb + 64, hp, 512:512 + NS], qap, True, True,
                       ["KTs", ("qT", hp, 2)], [("ps", pf)])
                    pni = rot("pn", 2)
                    act(Pnear[pni][:, 0:48], psum[pf][:, 0:48], AF.Exp, [("ps", pf), ("pn", pni), "cvec"], [("pn", pni)],
                        scale=0.125, bias=cvec[:, h:h + 1])
                    tmi = rot("tmp", NT)
                    dve((lambda pf, tmi, h: lambda e: e.scalar_tensor_tensor(
                        out=tmpf[tmi][:, 48:64], in0=psum[pf][:, 48:64], scalar=0.125, in1=Bt[:, 1, h, 0:NS],
                        op0=ALU.mult, op1=ALU.add))(pf, tmi, h), [("ps", pf), "Bt", ("tmp", tmi)], [("tmp", tmi)])
                    dve((lambda pf, tmi, h: lambda e: e.scalar_tensor_tensor(
                        out=tmpf[tmi][0:NS, 64:80], in0=psum[pf][0:NS, 64:80], scalar=0.125, in1=Bt[0:NS, 0, h, 0:NS],
                        op0=ALU.mult, op1=ALU.add))(pf, tmi, h), [("ps", pf), "Bt", ("tmp", tmi)], [("tmp", tmi)])
                    act(Pnear[pni][:, 48:64], tmpf[tmi][:, 48:64], AF.Exp, [("tmp", tmi), ("pn", pni)], [("pn", pni)])
                    act(Pnear[pni][0:NS, 64:80], tmpf[tmi][0:NS, 64:80], AF.Exp, [("tmp", tmi), ("pn", pni)], [("pn", pni)])
                    return pni

                def s_B(h, pni):
                    ob = oacc[h // 4]
                    osl = psum[ob][0:NS, (h % 4) * 65:(h % 4) * 65 + 65]
                    for r in range(4):
                        mm(osl, Pnear[pni][:, r * NS:(r + 1) * NS], V65s[:, r, h, :], r == 0, False,
                           [("pn", pni), "V65s"], [("ps", ob)])
                    mm(osl, Pnear[pni][0:NS, 64:80], V65s[0:NS, 4, h, :], False, True,
                       [("pn", pni), "V65s"], [("ps", ob)])

                cur = s_A(0)
                for h in range(8):
                    nxt = s_A(h + 1) if h + 1 < 8 else None
                    s_B(h, cur)
                    cur = nxt
                yi = rot("tmp", NT)
                for half in range(2):
                    ob = oacc[half]
                    o3 = psum[ob][0:NS, 0:260].rearrange("p (h e) -> p h e", h=4)
                    dve((lambda o3, half: lambda e: e.reciprocal(out=rden[0:NS, half * 4:half * 4 + 4], in_=o3[:, :, 64]))(o3, half),
                        [("ps", ob), "rden"], ["rden"])
                    dve((lambda o3, half, yi: lambda e: e.tensor_tensor(
                        out=yatok[yi][0:NS, half * 256:(half + 1) * 256].rearrange("p (h d) -> p h d", h=4),
                        in0=o3[:, :, 0:64],
                        in1=rden[0:NS, half * 4:half * 4 + 4].unsqueeze(2).to_broadcast([NS, 4, 64]),
                        op=ALU.mult))(o3, half, yi),
                        [("ps", ob), "rden", ("tmp", yi)], [("tmp", yi)])
                pinned.difference_update(oacc)
                pt = next_ps()
                for c in range(4):
                    T.op("pe", (lambda pt, c, yi: lambda e: e.transpose(
                        psum[pt][:, c * NS:(c + 1) * NS], yatok[yi][0:NS, c * 128:(c + 1) * 128], ident[0:NS, 0:NS]))(pt, c, yi),
                        reads=[("tmp", yi), "ident"], writes=[("ps", pt)])
                act(yaT[:, :, TB:TB + NS], psum[pt][:, 0:4 * NS].rearrange("p (c q) -> p c q", c=4), AF.Copy,
                    [("ps", pt)] + [("yaT", c, 2) for c in range(4)], [("yaT", c, 2) for c in range(4)])

            if MSTAGE <= 5:
                return
            if not last:
                dve(lambda e: e.tensor_copy(out=histK[l][:], in_=KT[:, :, 512:1024]),
                    [("KT", t) for t in range(4, 8)] + [("hK", l)], [("hK", l)])
                dve(lambda e: e.tensor_copy(out=histV[l][:], in_=V65[:, 4:8, :, :]),
                    [("V65", t) for t in range(4, 8)] + [("hV", l)], [("hV", l)])

            if MSTAGE <= 6:
                return
            KTk = [("KT", t) for t in range(8)]
            ycTk = [("ycT", c, ti) for c in range(4) for ti in range(3)]
            T.transfer(KTk, ycTk)
            wc0 = (l * 3) * 4
            for hc in range(2):
                icc, wcc = load_w(winv[:, :, 512 + hc * 256:512 + (hc + 1) * 256], (8, 256), "cc")
                icv, wcv = load_w(winv[:, :, 1024 + hc * 256:1024 + (hc + 1) * 256], (8, 256), "cv")
                icb, wcb = load_w(winv[:, :, hc * 256:(hc + 1) * 256], (8, 256), "cb")
                cs = (2 * hc, 2 * hc + 1)
                for (c0, w) in tiles:
                    ti = c0 // 512
                    is_s = (c0 == TB)
                    if is_s:
                        for c in cs:
                            T.dma("sp", (lambda c: lambda e: e.dma_start(
                                out=uT[:, c, 0:2], in_=sc[l][:, c * 128:(c + 1) * 128].rearrange("j p -> p j"),
                                allow_slow_non_contiguous=True))(c),
                                writes=["uT"], reads=["uT"], semkey="uhist")
                    else:
                        dve((lambda hc: lambda e: e.tensor_copy(out=uT[:, 2 * hc:2 * hc + 2, 0:2], in_=histU[l][:, 2 * hc:2 * hc + 2, :]))(hc),
                            [("hU", l), "uT"], ["uT"])
                    for c in cs:
                        cl = c - 2 * hc
                        pc = next_ps()
                        for k in range(8):
                            mm(psum[pc][:, 0:w], wcc[:, k, cl * 128:(cl + 1) * 128], hT[:, k, c0:c0 + w],
                               k == 0, k == 7, icc + [("hT", k, ti)], [("ps", pc)])
                        tmi = rot("tmp", NT)
                        act(tmpf[tmi][:, 0:w], psum[pc][:, 0:w], AF.Copy, [("ps", pc)], [("tmp", tmi)])
                        pv = next_ps()
                        for k in range(8):
                            mm(psum[pv][:, 0:w], wcv[:, k, cl * 128:(cl + 1) * 128], hT[:, k, c0:c0 + w],
                               k == 0, k == 7, icv + [("hT", k, ti)], [("ps", pv)])
                        dve((lambda c, w, pv, tmi: lambda e: e.tensor_tensor(
                            out=uT[:, c, 2:2 + w], in0=psum[pv][:, 0:w], in1=tmpf[tmi][:, 0:w], op=ALU.mult))(c, w, pv, tmi),
                            [("ps", pv), ("tmp", tmi), "uT"], ["uT"])
                        dve((lambda c, w: lambda e: e.tensor_scalar(
                            out=yconv[:, c, 0:w], in0=uT[:, c, 0:w], scalar1=wcT[:, wc0 + c: wc0 + c + 1],
                            scalar2=None, op0=ALU.mult))(c, w), ["uT", "wcT", "yconv"], ["yconv"])
                        for j in (1, 2):
                            dve((lambda c, w, j: lambda e: e.scalar_tensor_tensor(
                                out=yconv[:, c, 0:w], in0=uT[:, c, j:j + w],
                                scalar=wcT[:, wc0 + j * 4 + c: wc0 + j * 4 + c + 1],
                                in1=yconv[:, c, 0:w], op0=ALU.mult, op1=ALU.add))(c, w, j),
                                ["uT", "yconv", "wcT"], ["yconv"])
                        pb = next_ps()
                        for k in range(8):
                            mm(psum[pb][:, 0:w], wcb[:, k, cl * 128:(cl + 1) * 128], hT[:, k, c0:c0 + w],
                               k == 0, k == 7, icb + [("hT", k, ti)], [("ps", pb)])
                        dve((lambda c, c0, w, pb: lambda e: e.tensor_tensor(
                            out=ycT[:, c, c0:c0 + w], in0=psum[pb][:, 0:w], in1=yconv[:, c, 0:w], op=ALU.mult))(c, c0, w, pb),
                            [("ps", pb), "yconv"], [("ycT", c, ti)])
                    if is_s:
                        dve((lambda hc: lambda e: e.tensor_copy(out=ust[:, 2 * hc:2 * hc + 2, :], in_=uT[:, 2 * hc:2 * hc + 2, NS:NS + 2]))(hc),
                            ["uT", "ust"], ["ust"])
                        for c in cs:
                            T.dma("sp", (lambda c: lambda e: e.dma_start(out=ocs[l][:, c * 128:(c + 1) * 128].rearrange("j p -> p j"), in_=ust[:, c, :], allow_slow_non_contiguous=True))(c),
                                  reads=["ust"], semkey="ust")
                    else:
                        dve((lambda hc: lambda e: e.tensor_copy(out=histU[l][:, 2 * hc:2 * hc + 2, :], in_=uT[:, 2 * hc:2 * hc + 2, 512:514]))(hc),
                            ["uT", ("hU", l)], [("hU", l)])
                        if last and c0 == 512:
                            dve((lambda hc: lambda e: e.tensor_copy(out=ust[:, 2 * hc:2 * hc + 2, :], in_=uT[:, 2 * hc:2 * hc + 2, 512:514]))(hc),
                                ["uT", "ust"], ["ust"])
                            for c in cs:
                                T.dma("sp", (lambda c: lambda e: e.dma_start(out=ocp[l][:, c * 128:(c + 1) * 128].rearrange("j p -> p j"), in_=ust[:, c, :], allow_slow_non_contiguous=True))(c),
                                      reads=["ust"], semkey="ust")

            if MSTAGE <= 7:
                return
            T.transfer(["uT", "yconv"], mTk)
            for og in range(4):
                iwco, wcou = load_w(wco[l].rearrange("(c p) n -> p c n", p=128)[:, :, og * 256:(og + 1) * 256], (4, 256), "wco")
                iwao, waou = load_w(wao[l].rearrange("(c p) n -> p c n", p=128)[:, :, og * 256:(og + 1) * 256], (4, 256), "wao")
                igc, wgc = load_w(winv[:, :, 3072 + og * 256: 3072 + (og + 1) * 256], (8, 256), "gc")
                iga, wga = load_w(winv[:, :, 4096 + og * 256: 4096 + (og + 1) * 256], (8, 256), "ga")
                for (oo, (c0, w)) in ([(oo, t) for oo in range(2) for t in tiles[:2]] +
                                      [(oo, t) for oo in range(2) for t in tiles[2:]]):
                    o = og * 2 + oo
                    if True:
                        ti = c0 // 512
                        res = []
                        for (wg_, ig_, wp_, ip_, src, skey, bcol) in (
                                (wgc, igc, wcou, iwco, ycT, "ycT", l * 16 + o),
                                (wga, iga, waou, iwao, yaT, "yaT", l * 16 + 8 + o)):
                            pg = next_ps()
                            for k in range(8):
                                mm(psum[pg][:, 0:w], wg_[:, k, oo * 128:(oo + 1) * 128], hT[:, k, c0:c0 + w],
                                   k == 0, k == 7, ig_ + [("hT", k, ti)], [("ps", pg)])
                            tmi = rot("tmp", NT)
                            act(tmpf[tmi][:, 0:w], psum[pg][:, 0:w], AF.Sigmoid, [("ps", pg), ("tmp", tmi), "bgT"], [("tmp", tmi)],
                                bias=bgT[:, bcol:bcol + 1])
                            pp = next_ps()
                            for c in range(4):
                                mm(psum[pp][:, 0:w], wp_[:, c, oo * 128:(oo + 1) * 128], src[:, c, c0:c0 + w],
                                   c == 0, c == 3, ip_ + [(skey, c, ti)], [("ps", pp)])
                            dve((lambda tmi, pp, w: lambda e: e.tensor_tensor(
                                out=tmpf[tmi][:, 0:w], in0=psum[pp][:, 0:w], in1=tmpf[tmi][:, 0:w], op=ALU.mult))(tmi, pp, w),
                                [("ps", pp), ("tmp", tmi)], [("tmp", tmi)])
                            res.append(tmi)
                        dve((lambda o, c0, w, r0, r1: lambda e: e.tensor_tensor(
                            out=mT[:, o, c0:c0 + w], in0=tmpf[r0][:, 0:w], in1=tmpf[r1][:, 0:w], op=ALU.add))(o, c0, w, res[0], res[1]),
                            [("tmp", res[0]), ("tmp", res[1])], [("mT", o, ti)])

            if MSTAGE <= 8:
                return
            tst = TailStats(b, tail_apply) if tail else None
            wos = []
            for og in range(4):
                wos.append(load_w(wo[l].rearrange("(k p) n -> p k n", p=128)[:, :, og * 256:(og + 1) * 256], (8, 256), "wo"))
            for (c0, w) in tiles:
                ti = c0 // 512
                for og in range(4):
                    iwo, wou = wos[og]
                    for m in range(2):
                        o = og * 2 + m
                        pi = next_ps()
                        for k in range(8):
                            mm(psum[pi][:, 0:w], wou[:, k, m * 128:(m + 1) * 128], mT[:, k, c0:c0 + w],
                               k == 0, k == 7, iwo + [("mT", k, ti)], [("ps", pi)])
                        dve((lambda o, c0, w, pi: lambda e: e.tensor_tensor(
                            out=xT[:, o, c0:c0 + w], in0=psum[pi][:, 0:w], in1=xT[:, o, c0:c0 + w], op=ALU.add))(o, c0, w, pi),
                            [("ps", pi), ("xT", o, ti)], [("xT", o, ti)])
                        if tst is not None:
                            tst.chunk_done(o, c0, w)
            return tst.finish() if tst is not None else None

        def debug_dump():
            for (buf, key, nch, row0) in ((ycT, "ycT", 4, 1024), (yaT, "yaT", 4, 1536), (mT, "mT", 8, 2048)):
                T.dma("pool", (lambda buf, nch, row0: lambda e: e.dma_start(
                    out=yp[row0:row0 + nch * 128, :].rearrange("(c p) t -> p c t", p=128),
                    in_=buf[:, 0:nch, 0:1024]))(buf, nch, row0),
                    reads=[(key, c, ti) for c in range(nch) for ti in range(3)], semkey="dbgdump")

        def final_store(b, pre=None):
            rmsnorm_final(b, pre)

        def rmsnorm_final(b, pre=None):
            for (c0, w) in tiles_of(b):
                ti = c0 // 512
                if not (pre is not None and pre[ti] is None):
                    ri = pre[ti] if pre is not None else rms_stats(c0, w, ti)
                    emit_final_scale(c0, w, ti, ri)
                nsub = (w + 127) // 128
                for s_ in range(nsub):
                    n = min(128, w - s_ * 128)
                    col = c0 + s_ * 128
                    for half in range(2):
                        oi = rot("xst", 4)
                        pi = next_ps()
                        for jj in range(4):
                            k = half * 4 + jj
                            T.op("pe", (lambda pi, jj, k, col, n: lambda e: e.transpose(
                                psum[pi][0:n, jj * 128:(jj + 1) * 128], xT[:, k, col:col + n], ident[:]))(pi, jj, k, col, n),
                                reads=[("xT", k, ti), "ident"], writes=[("ps", pi)])
                        if half == 0:
                            dve((lambda oi, pi, n: lambda e: e.tensor_copy(out=ostage[oi][0:n, :], in_=psum[pi][0:n, :]))(oi, pi, n),
                                [("ps", pi), ("xst", oi)], [("xst", oi)])
                        else:
                            act(ostage[oi][0:n, :], psum[pi][0:n, :], AF.Copy, [("ps", pi), ("xst", oi)], [("xst", oi)])
                        if c0 == TB:
                            dst = ys[:, half * 512:(half + 1) * 512]
                        else:
                            dst = yp[b * TB + col: b * TB + col + n, half * 512:(half + 1) * 512]
                        T.dma("sp", (lambda oi, dst, n: lambda e: e.dma_start(out=dst, in_=ostage[oi][0:n, :]))(oi, dst, n),
                              reads=[("xst", oi)], semkey=("xst", oi))

        STAGE = int(os.environ.get("K_STAGE", 9))
        for b in range(DBG_NBLK):
            if STAGE >= 1:
                load_x_block(b)
            pre = None
            full = STAGE >= 5
            for l in range(DBG_NLAY):
                if STAGE >= 2:
                    def hook(l=l):
                        build_bias(l)
                        bias_prebuilt[0] = True
                    pre = ffn(b, l, w1gu, w1d, 0 + l * 8, pre=pre, tail=full, mid_hook=hook if STAGE >= 3 else None,
                              tail_apply=emit_hT(32 + l * 8) if full else None)
                if STAGE >= 3:
                    pre = mixer(b, l, pre=pre, tail=full, tail_apply=emit_hT(64 + l * 8) if full else None)
                    if os.environ.get("K_DUMP"):
                        debug_dump()
                if STAGE >= 4:
                    nxt_apply = None
                    if full:
                        nxt_apply = emit_hT((l + 1) * 8) if l + 1 < DBG_NLAY else emit_final_scale
                    pre = ffn(b, l, w2gu, w2d, 64 + l * 8, pre=pre, tail=full, tail_apply=nxt_apply)
            if STAGE >= 5 or os.environ.get("K_FINAL"):
                final_store(b, pre)
        T.final_wait("sp")

        with nc.Block() as block:
            @block.tensor
            def _(e):
                T.replay("pe", e)

            @block.scalar
            def _(e):
                T.replay("act", e)

            @block.vector
            def _(e):
                T.replay("dve", e)

            @block.gpsimd
            def _(e):
                T.replay("pool", e)

            @block.sync
            def _(e):
                T.replay("sp", e)
        print("kernel: %d instructions, sems=%d" % (T.ninstr, len(T.dsem) + 5))
    return nc


_NC_CACHE = {}


def kernel(x_prompt, x_sample, cache_k, cache_v, state_conv,
           norm_ffn1, w_ffn1_gu, w_ffn1_down, norm_mix, w_in, b_gate, w_conv,
           rel_bias, w_conv_out, w_attn_out, w_o, norm_ffn2, w_ffn2_gu, w_ffn2_down,
           norm_final):
    f = lambda a: np.ascontiguousarray(np.asarray(a, dtype=np.float32))
    if "nc" not in _NC_CACHE:
        _NC_CACHE["nc"] = build_program()
    nc = _NC_CACHE["nc"]
    shared = {
        "norm_ffn1": f(norm_ffn1), "w_ffn1_gu": f(w_ffn1_gu), "w_ffn1_down": f(w_ffn1_down),
        "norm_mix": f(norm_mix), "w_in": f(w_in), "b_gate": f(b_gate), "w_conv": f(w_conv),
        "rel_bias": f(rel_bias), "w_conv_out": f(w_conv_out), "w_attn_out": f(w_attn_out),
        "w_o": f(w_o), "norm_ffn2": f(norm_ffn2), "w_ffn2_gu": f(w_ffn2_gu),
        "w_ffn2_down": f(w_ffn2_down), "norm_final": f(norm_final),
    }
    x_prompt = np.asarray(x_prompt); x_sample = np.asarray(x_sample)
    cache_k = np.asarray(cache_k); cache_v = np.asarray(cache_v); state_conv = np.asarray(state_conv)
    in_maps = []
    for c in range(8):
        m = dict(shared)
        m["xp"] = f(x_prompt[c])
        m["xs"] = f(x_sample[c])
        m["ck"] = f(cache_k[:, c].reshape(DEPTH, 512, 512))
        m["cv"] = f(cache_v[:, c].reshape(DEPTH, 512, 512))
        m["sc"] = f(state_conv[:, c])
        in_maps.append(m)
    res = run_bass_kernel_spmd(nc, in_maps, core_ids=list(range(8)))
    R = res.results
    y_prompt = np.stack([R[c]["yp"] for c in range(8)], 0).astype(np.float32)
    y_sample = np.stack([R[c]["ys"] for c in range(8)], 0).astype(np.float32)
    nkp = np.stack([R[c]["okp"].reshape(DEPTH, 512, 8, 64) for c in range(8)], 1).astype(np.float32)
    nvp = np.stack([R[c]["ovp"].reshape(DEPTH, 512, 8, 64) for c in range(8)], 1).astype(np.float32)
    ncp = np.stack([R[c]["ocp"] for c in range(8)], 1).astype(np.float32)
    nks = np.stack([R[c]["oks"].reshape(DEPTH, NS, 8, 64) for c in range(8)], 1).astype(np.float32)
    nvs = np.stack([R[c]["ovs"].reshape(DEPTH, NS, 8, 64) for c in range(8)], 1).astype(np.float32)
    ncs = np.stack([R[c]["ocs"] for c in range(8)], 1).astype(np.float32)
    return (y_prompt, y_sample, nkp, nvp, ncp, nks, nvs, ncs)
```

```python
import os
from contextlib import ExitStack
import numpy as np
import concourse.bass as bass
import concourse.mybir as mybir
from concourse.bass_utils import run_bass_kernel_spmd

F32 = mybir.dt.float32
BF16 = mybir.dt.bfloat16
AF = mybir.ActivationFunctionType
ALU = mybir.AluOpType

D = 1024
DFF = 2816
DEPTH = 4
SEQ = 4096
TB = 1024
NBLK = SEQ // TB
NS = 16
NCOL = TB + NS
INC = 5120
EPS = 1e-6
NEG = -30000.0
NWSLOT = 8
NT = 6

DBG_NBLK = int(os.environ.get("K_NBLK", NBLK))
DBG_NLAY = int(os.environ.get("K_NLAY", DEPTH))
MSTAGE = int(os.environ.get("K_MSTAGE", 99))
VST = int(os.environ.get("K_VST", 99))


class Tracker:
    def __init__(self, nc, es):
        self.nc = nc
        self.es = es
        self.engs = ["pe", "act", "dve", "pool", "sp"]
        self.sem = {e: es.enter_context(nc.semaphore("s_" + e)) for e in self.engs}
        self.cnt = {e: 0 for e in self.engs}
        self.lists = {e: [] for e in self.engs}
        self.clock = {e: {} for e in self.engs}
        self.lastw = {}
        self.reads = {}
        self.dsem = {}
        self.dcnt = {}
        self.ninstr = 0

    def _dma_sem(self, key):
        if key not in self.dsem:
            self.dsem[key] = self.es.enter_context(self.nc.semaphore("d_%d" % len(self.dsem)))
            self.dcnt[key] = 0
        return key

    def _semobj(self, sid):
        return self.sem[sid] if sid in self.sem else self.dsem[sid]

    def _deps(self, eng, reads, writes, pe_acc=False):
        need = {}

        def add(ev):
            if ev is None:
                return
            s, v = ev
            if need.get(s, 0) < v:
                need[s] = v
        for k in reads:
            add(self.lastw.get(k))
        for k in writes:
            add(self.lastw.get(k))
            for ev in self.reads.get(k, {}).items():
                add(ev)
        waits = []
        ck = self.clock[eng]
        for s, v in need.items():
            if s == "pe" and eng == "pe":
                continue
            if ck.get(s, 0) >= v:
                continue
            ck[s] = v
            waits.append((s, v))
        return waits

    def _commit(self, ev, reads, writes):
        s, v = ev
        for k in reads:
            r = self.reads.setdefault(k, {})
            if r.get(s, 0) < v:
                r[s] = v
        for k in writes:
            self.lastw[k] = ev
            self.reads[k] = {}

    @staticmethod
    def _psum_excl(reads, writes):
        pr = [k for k in reads if isinstance(k, tuple) and k[0] == "ps"]
        if not pr:
            return list(reads), list(writes)
        return [k for k in reads if k not in pr], list(writes) + [k for k in pr if k not in writes]

    def op(self, eng, fn, reads=(), writes=()):
        reads, writes = self._psum_excl(reads, writes)
        waits = self._deps(eng, reads, writes)
        self.cnt[eng] += 1
        ev = (eng, self.cnt[eng])
        self.lists[eng].append((waits, fn, eng, 1))
        self._commit(ev, reads, writes)
        self.ninstr += 1

    def dma(self, q, fn, reads=(), writes=(), semkey=None):
        key = self._dma_sem(semkey)
        waits = self._deps(q, reads, writes)
        prev = self.dcnt[key]
        if prev and self.clock[q].get(key, 0) < prev:
            self.clock[q][key] = prev
            waits.append((key, prev))
        self.dcnt[key] += 16
        ev = (key, self.dcnt[key])
        self.lists[q].append((waits, fn, key, 16))
        self._commit(ev, reads, writes)
        self.ninstr += 1

    def transfer(self, old_keys, new_keys):
        evs = {}
        for k in old_keys:
            ev = self.lastw.get(k)
            if ev is not None and evs.get(ev[0], 0) < ev[1]:
                evs[ev[0]] = ev[1]
            for s_, v in self.reads.get(k, {}).items():
                if evs.get(s_, 0) < v:
                    evs[s_] = v
        for k in new_keys:
            r = self.reads.setdefault(k, {})
            for s_, v in evs.items():
                if r.get(s_, 0) < v:
                    r[s_] = v

    def final_wait(self, q):
        waits = []
        for key, v in self.dcnt.items():
            if v and self.clock[q].get(key, 0) < v:
                waits.append((key, v))
        for e in self.engs:
            if e != q and self.cnt[e]:
                waits.append((e, self.cnt[e]))
        self.lists[q].append((waits, None, None, 0))

    def replay(self, eng, e):
        for waits, fn, sid, inc in self.lists[eng]:
            for s, v in waits:
                e.wait_ge(self._semobj(s), v)
            if fn is not None:
                ins = fn(e)
                ins.then_inc(self._semobj(sid), inc)


def build_program():
    nc = bass.Bass("TRN2", target_bir_lowering=False)
    es = ExitStack()
    with es:
        T = Tracker(nc, es)

        def dram_in(name, shape):
            return nc.dram_tensor(name, list(shape), F32, kind="ExternalInput")

        def dram_out(name, shape):
            return nc.dram_tensor(name, list(shape), F32, kind="ExternalOutput")

        xp = dram_in("xp", [SEQ, D]).ap()
        xs = dram_in("xs", [NS, D]).ap()
        ck = dram_in("ck", [DEPTH, 512, 512]).ap()
        cv = dram_in("cv", [DEPTH, 512, 512]).ap()
        sc = dram_in("sc", [DEPTH, 2, 512]).ap()
        n1 = dram_in("norm_ffn1", [DEPTH, D]).ap()
        w1gu = dram_in("w_ffn1_gu", [DEPTH, D, 2 * DFF]).ap()
        w1d = dram_in("w_ffn1_down", [DEPTH, DFF, D]).ap()
        nm = dram_in("norm_mix", [DEPTH, D]).ap()
        win = dram_in("w_in", [DEPTH, D, INC]).ap()
        bgate = dram_in("b_gate", [DEPTH, 2 * D]).ap()
        wconv = dram_in("w_conv", [DEPTH, 3, 512]).ap()
        relb_t = dram_in("rel_bias", [DEPTH, 8, 257])
        relb = relb_t.ap()
        wco = dram_in("w_conv_out", [DEPTH, 512, D]).ap()
        wao = dram_in("w_attn_out", [DEPTH, 512, D]).ap()
        wo = dram_in("w_o", [DEPTH, D, D]).ap()
        n2 = dram_in("norm_ffn2", [DEPTH, D]).ap()
        w2gu = dram_in("w_ffn2_gu", [DEPTH, D, 2 * DFF]).ap()
        w2d = dram_in("w_ffn2_down", [DEPTH, DFF, D]).ap()
        nf = dram_in("norm_final", [D]).ap()

        yp = dram_out("yp", [SEQ, D]).ap()
        ys = dram_out("ys", [NS, D]).ap()
        okp = dram_out("okp", [DEPTH, 512, 512]).ap()
        ovp = dram_out("ovp", [DEPTH, 512, 512]).ap()
        ocp = dram_out("ocp", [DEPTH, 2, 512]).ap()
        oks = dram_out("oks", [DEPTH, NS, 512]).ap()
        ovs = dram_out("ovs", [DEPTH, NS, 512]).ap()
        ocs = dram_out("ocs", [DEPTH, 2, 512]).ap()
        ext_t = nc.dram_tensor("ext_scratch", [DEPTH, 8, 384], F32, kind="Internal")
        ext = ext_t.ap()

        def sb(name, shape, dt=F32):
            return es.enter_context(nc.sbuf_tensor(name, list(shape), dt))

        xT = sb("xT", [128, 8, NCOL])
        hT = sb("hT", [128, 8, NCOL], BF16)
        wbig = sb("wbig", [128, NWSLOT * 2048], BF16)
        histK = [sb("hK%d" % l, [128, 4, 512], BF16) for l in range(DEPTH)]
        histV = [sb("hV%d" % l, [128, 4, 8, 65], BF16) for l in range(DEPTH)]
        histU = [sb("hU%d" % l, [128, 4, 2]) for l in range(DEPTH)]
        gains = sb("gains", [128, 104])
        bgT = sb("bgT", [128, 64])
        wcT = sb("wcT", [128, 48])
        ident = sb("ident", [128, 128])
        antiI = sb("antiI", [128, 128])
        ones_f = sb("ones_f", [128, 128])
        ones_b = sb("ones_b", [128, 128], BF16)
        cstage = sb("cstage", [128, 2, 128])
        Bt = sb("Bt", [128, 2, 8, 128])
        cvec = sb("cvec", [128, 8])
        xstage = [sb("xst%d" % i, [128, 512]) for i in range(4)]
        ostage = xstage
        tmpf = [sb("tmpf%d" % i, [128, 512]) for i in range(NT)]
        aT = [sb("aT%d" % i, [128, 4, NCOL], BF16) for i in range(2)]
        qT = aT[0]
        yaT = aT[1]
        arenaC = sb("arenaC", [128, 4 * NCOL])
        mT = arenaC[:].bitcast(BF16).rearrange("p (o t) -> p o t", o=8)
        uT = arenaC[:, 0:4 * 514].rearrange("p (c t) -> p c t", c=4)
        yconv = arenaC[:, 4 * 514:4 * 514 + 2048].rearrange("p (c t) -> p c t", c=4)
        KTb = sb("KTb", [128, 4, NCOL], BF16)
        KT = KTb[:, :, 0:TB]
        ycT = KTb
        V65 = sb("V65", [128, 8, 8, 65], BF16)
        KTs = sb("KTs", [128, 4, 512 + NS], BF16)
        V65s = sb("V65s", [128, 5, 8, 65], BF16)
        Pfar = [sb("Pfar%d" % i, [128, 384], BF16) for i in range(4)]
        Pnear = [sb("Pnear%d" % i, [128, 256], BF16) for i in range(2)]
        yatok = tmpf
        kvst = tmpf
        rden = sb("rden", [128, 8])
        ust = sb("ust", [128, 4, 2])

        psum = [es.enter_context(nc.psum_tensor("ps%d" % i, [128, 512], F32)) for i in range(8)]
        st = {"ps": 0, "w": 0, "tmp": 0, "xst": 0, "pf": 0, "pn": 0, "kv": 0, "ya": 0}

        pinned = set()

        def next_ps():
            i = st["ps"]
            while i in pinned:
                i = (i + 1) % 8
            st["ps"] = (i + 1) % 8
            return i

        def rot(name, n):
            i = st[name]
            st[name] = (i + 1) % n
            return i

        def mm(out, lhsT, rhs, start, stop, reads, writes):
            T.op("pe", lambda e: e.matmul(out, lhsT=lhsT, rhs=rhs, start=start, stop=stop),
                 reads=reads, writes=writes)

        def act(out, in_, func, reads, writes, **kw):
            T.op("act", lambda e: e.activation(out=out, in_=in_, func=func, **kw), reads=reads, writes=writes)

        def dve(fn, reads, writes):
            T.op("dve", fn, reads=reads, writes=writes)

        def load_w(src_ap, shape3, name, slot=None):
            a, b_ = shape3
            ns = (a * b_ + 2047) // 2048
            if slot is not None:
                i = slot
            else:
                i = st["w"]
                if ns == 2 and i % 2 == 1:
                    i = (i + 1) % NWSLOT
                st["w"] = (i + ns) % NWSLOT
            keys = [("w", i + j) for j in range(ns)]
            dst = wbig[:, i * 2048: i * 2048 + a * b_].rearrange("p (a b) -> p a b", a=a)
            T.dma("pool", lambda e: e.dma_start(out=dst, in_=src_ap), reads=[], writes=keys,
                  semkey=("w", i))
            return keys, dst

        T.op("pool", lambda e: e.memset(ones_f[:], 1.0), writes=["ones_f"])
        T.op("pool", lambda e: e.memset(ones_b[:], 1.0), writes=["ones_b"])
        T.op("pool", lambda e: e.affine_select(out=ident[:], in_=ones_f[:], pattern=[[-1, 128]],
                                               compare_op=ALU.is_equal, fill=0.0, base=0,
                                               channel_multiplier=1),
             reads=["ones_f"], writes=["ident"])
        T.op("pool", lambda e: e.affine_select(out=antiI[:], in_=ones_f[:], pattern=[[1, 128]],
                                               compare_op=ALU.is_equal, fill=0.0, base=-127,
                                               channel_multiplier=1),
             reads=["ones_f"], writes=["antiI"])
        for i in range(4):
            T.op("pool", (lambda i: lambda e: e.memset(Pfar[i][:], 0.0))(i), writes=[("pf", i)])
        for l in range(DEPTH):
            T.op("pool", (lambda l: lambda e: e.memset(histV[l][:], 1.0))(l), writes=[("hV", l)])
            T.op("pool", (lambda l: lambda e: e.memset(histU[l][:], 0.0))(l), writes=[("hU", l)])
        T.op("pool", lambda e: e.memset(V65[:], 1.0), writes=[("V65", t) for t in range(8)])
        T.op("pool", lambda e: e.memset(V65s[:], 1.0), writes=["V65s"])

        T.op("pool", lambda e: e.memset(cstage[:], 0.0), writes=["cstage"])
        T.dma("sp", lambda e: e.dma_start(out=cstage[0:32, 0, :], in_=n1.rearrange("l (k p) -> (l k) p", p=128)),
              writes=["cstage"], reads=["cstage"], semkey="c0")
        T.dma("sp", lambda e: e.dma_start(out=cstage[32:64, 0, :], in_=nm.rearrange("l (k p) -> (l k) p", p=128)),
              writes=["cstage"], reads=["cstage"], semkey="c1")
        T.dma("sp", lambda e: e.dma_start(out=cstage[64:96, 0, :], in_=n2.rearrange("l (k p) -> (l k) p", p=128)),
              writes=["cstage"], reads=["cstage"], semkey="c2")
        T.dma("sp", lambda e: e.dma_start(out=cstage[96:104, 0, :], in_=nf.rearrange("(k p) -> k p", p=128)),
              writes=["cstage"], reads=["cstage"], semkey="c3")
        T.dma("sp", lambda e: e.dma_start(out=cstage[0:64, 1, :], in_=bgate.rearrange("l (k p) -> (l k) p", p=128)),
              writes=["cstage"], reads=["cstage"], semkey="c4")
        T.dma("sp", lambda e: e.dma_start(out=cstage[64:112, 1, :],
                                          in_=wconv.rearrange("l j (k p) -> (l j k) p", p=128)),
              writes=["cstage"], reads=["cstage"], semkey="c5")
        p0 = next_ps()
        T.op("pe", lambda e: e.transpose(psum[p0][:, 0:128], cstage[:, 0, :], ident[:]),
             reads=["cstage", "ident"], writes=[("ps", p0)])
        T.op("pe", lambda e: e.transpose(psum[p0][:, 128:256], cstage[:, 1, :], ident[:]),
             reads=["cstage", "ident"], writes=[("ps", p0)])
        dve(lambda e: e.tensor_copy(out=gains[:], in_=psum[p0][:, 0:104]), [("ps", p0)], ["gains"])
        dve(lambda e: e.tensor_copy(out=bgT[:], in_=psum[p0][:, 128:192]), [("ps", p0)], ["bgT"])
        dve(lambda e: e.tensor_copy(out=wcT[:], in_=psum[p0][:, 192:240]), [("ps", p0)], ["wcT"])

        T.dma("sp", lambda e: e.dma_start(out=ext[:, :, 0:257], in_=relb), writes=["ext"], semkey="e0")
        rbs = sb("rbs", [32, 1])
        rbb = sb("rbb", [32, 127])
        T.dma("sp", lambda e: e.dma_start(out=rbs[:], in_=bass.AP(relb_t, 256, [[257, 32], [1, 1]]),
                                          allow_slow_non_contiguous=True),
              writes=["rbs"], semkey="e1")
        dve(lambda e: e.tensor_copy(out=rbb[:], in_=rbs[:, 0:1].to_broadcast([32, 127])), ["rbs"], ["rbb"])
        T.dma("sp", lambda e: e.dma_start(out=ext.rearrange("l h i -> (l h) i")[:, 257:384], in_=rbb[:]),
              writes=["ext"], reads=["ext", "rbb"], semkey="e2")

        def tiles_of(b):
            t = [(0, 512), (512, 512)]
            if b == 0:
                t.append((TB, NS))
            return t

        def load_x_block(b):
            tiles = []
            for tt in range(8):
                tiles.append((xp[b * TB + tt * 128: b * TB + (tt + 1) * 128, :], 128, tt * 128))
            if b == 0:
                tiles.append((xs, NS, TB))
            for src, n, col in tiles:
                for half in range(2):
                    si = rot("xst", 4)
                    T.dma("sp", (lambda si, src, n, half: lambda e: e.dma_start(
                        out=xstage[si][0:n, :], in_=src[:, half * 512:(half + 1) * 512]))(si, src, n, half),
                        writes=[("xst", si)], semkey=("xst", si))
                    pi = next_ps()
                    for j in range(4):
                        T.op("pe", (lambda pi, j, si, n: lambda e: e.transpose(
                            psum[pi][:, j * 128: j * 128 + n], xstage[si][0:n, j * 128:(j + 1) * 128],
                            ident[0:n, 0:n]))(pi, j, si, n),
                            reads=[("xst", si), "ident"], writes=[("ps", pi)])
                    srcp = psum[pi][:].rearrange("p (j t) -> p j t", j=4)[:, :, 0:n]
                    dstx = xT[:, half * 4: half * 4 + 4, col: col + n]
                    wk = [("xT", kk, col // 512) for kk in range(half * 4, half * 4 + 4)]
                    if half == 0:
                        dve((lambda dstx, srcp: lambda e: e.tensor_copy(out=dstx, in_=srcp))(dstx, srcp),
                            [("ps", pi)] + wk, wk)
                    else:
                        act(dstx, srcp, AF.Copy, [("ps", pi)] + wk, wk)

        def rms_stats(c0, w, ti):
            pi = next_ps()
            for k in range(8):
                qi = rot("tmp", NT)
                sqv = tmpf[qi][:].bitcast(BF16)[:, 0:w]
                act(sqv, xT[:, k, c0:c0 + w], AF.Square, [("xT", k, ti), ("tmp", qi)], [("tmp", qi)])
                mm(psum[pi][:, 0:w], ones_b[:], sqv, k == 0, k == 7, [("tmp", qi), "ones_b"], [("ps", pi)])
            si = rot("tmp", NT)
            act(tmpf[si][:, 0:w], psum[pi][:, 0:w], AF.Ln, [("ps", pi), ("tmp", si), "epsv"], [("tmp", si)],
                scale=1.0 / D, bias=epsv[:, 0:1])
            ri = rot("tmp", NT)
            act(tmpf[ri][:, 0:w], tmpf[si][:, 0:w], AF.Exp, [("tmp", si), ("tmp", ri)], [("tmp", ri)], scale=-0.5)
            return ri

        class TailStats:
            def __init__(self, b):
                act(dummy[0:1, 0:1], epsv[0:1, 0:1], AF.Ln, ["epsv", "dummy"], ["dummy"])
                self.tiles = tiles_of(b)
                self.bank = {}
                self.pend = []
                self.cnt = {}
                for (c0, w) in self.tiles:
                    ti = c0 // 512
                    self.bank[ti] = next_ps()
                    pinned.add(self.bank[ti])
                    self.cnt[ti] = 0

            def _flush(self, keep):
                while len(self.pend) > keep:
                    (ti, w, qi) = self.pend.pop(0)
                    n = self.cnt[ti]
                    self.cnt[ti] += 1
                    pi = self.bank[ti]
                    sqv = tmpf[qi][:].bitcast(BF16)[:, 0:w]
                    mm(psum[pi][:, 0:w], ones_b[:], sqv, n == 0, n == 7, [("tmp", qi), "ones_b"], [("ps", pi)])

            def chunk_done(self, k, c0, w):
                ti = c0 // 512
                qi = rot("tmp", NT)
                sqv = tmpf[qi][:].bitcast(BF16)[:, 0:w]
                act(sqv, xT[:, k, c0:c0 + w], AF.Square, [("xT", k, ti), ("tmp", qi)], [("tmp", qi)])
                self.pend.append((ti, w, qi))
                self._flush(2)

            def finish(self):
                self._flush(0)
                out = {}
                for (c0, w) in self.tiles:
                    ti = c0 // 512
                    pi = self.bank[ti]
                    si = rot("tmp", NT)
                    act(tmpf[si][:, 0:w], psum[pi][:, 0:w], AF.Ln, [("ps", pi), ("tmp", si), "epsv"], [("tmp", si)],
                        scale=1.0 / D, bias=epsv[:, 0:1])
                    ri = rot("tmp", NT)
                    act(tmpf[ri][:, 0:w], tmpf[si][:, 0:w], AF.Exp, [("tmp", si), ("tmp", ri)], [("tmp", ri)], scale=-0.5)
                    out[ti] = ri
                    pinned.discard(pi)
                return out

        def rmsnorm_to_hT(b, gidx, pre=None):
            for (c0, w) in tiles_of(b):
                ti = c0 // 512
                ri = pre[ti] if pre is not None else rms_stats(c0, w, ti)
                for k in range(8):
                    dve((lambda k, c0, w, ri: lambda e: e.scalar_tensor_tensor(
                        out=hT[:, k, c0:c0 + w], in0=xT[:, k, c0:c0 + w],
                        scalar=gains[:, gidx + k: gidx + k + 1], in1=tmpf[ri][:, 0:w],
                        op0=ALU.mult, op1=ALU.mult))(k, c0, w, ri),
                        [("xT", k, ti), ("tmp", ri), "gains"], [("hT", k, ti)])

        epsv = sb("epsv", [128, 1])
        dummy = sb("lnwarm", [128, 1])
        T.op("pool", lambda e: e.memset(epsv[:], EPS), writes=["epsv"])

        def ffn(b, l, wgu, wd, gidx, pre=None, tail=False, mid_hook=None):
            T.transfer([("qT", m, ti) for m in range(4) for ti in range(3)],
                       [("aT", 0, fc, ti) for fc in range(4) for ti in range(3)])
            T.transfer([("yaT", m, ti) for m in range(4) for ti in range(3)],
                       [("aT", 1, fc, ti) for fc in range(4) for ti in range(3)])
            rmsnorm_to_hT(b, gidx, pre)
            act(dummy[0:1, 0:1], epsv[0:1, 0:1], AF.Silu, ["epsv", "dummy"], ["dummy"])
            tiles = tiles_of(b)
            tstats = [None]
            NSUB = 11
            sgs = [(0,), (1, 2), (3, 4), (5, 6), (7, 8), (9, 10)]
            wguv = wgu[l].rearrange("(k p) n -> p k n", p=128)
            wdv = wd[l]
            loaded = {}

            def gu_units(si):
                ab = si % 2
                units = []
                for sub_i, n in enumerate(sgs[si]):
                    f0 = n * 256
                    holder = {}

                    def loads(n=n, f0=f0, holder=holder):
                        holder["g"] = load_w(wguv[:, :, f0:f0 + 256], (8, 256), "wg", slot=(2 * n) % 4)
                        holder["u"] = load_w(wguv[:, :, DFF + f0: DFF + f0 + 256], (8, 256), "wu", slot=(2 * n + 1) % 4)
                        loaded[n] = load_w(wdv[f0:f0 + 256, :].rearrange("(c p) n -> p c n", p=128), (2, D), "wd",
                                           slot=4 + n % 4)
                    first = True
                    for (f, (c0, w)) in ([(f, t) for f in range(2) for t in tiles[:2]] +
                                         [(f, t) for f in range(2) for t in tiles[2:]]):
                        fc = sub_i * 2 + f

                        def unit(f=f, c0=c0, w=w, fc=fc, holder=holder, pre=(loads if first else None)):
                            if pre is not None:
                                pre()
                            ig, wg = holder["g"]
                            iu, wu = holder["u"]
                            ti = c0 // 512
                            pg = next_ps()
                            for k in range(8):
                                mm(psum[pg][:, 0:w], wg[:, k, f * 128:(f + 1) * 128], hT[:, k, c0:c0 + w],
                                   k == 0, k == 7, ig + [("hT", k, ti)], [("ps", pg)])
                            pu = next_ps()
                            for k in range(8):
                                mm(psum[pu][:, 0:w], wu[:, k, f * 128:(f + 1) * 128], hT[:, k, c0:c0 + w],
                                   k == 0, k == 7, iu + [("hT", k, ti)], [("ps", pu)])
                            tmi = rot("tmp", NT)
                            act(tmpf[tmi][:, 0:w], psum[pg][:, 0:w], AF.Silu, [("ps", pg)], [("tmp", tmi)])
                            dve((lambda ab, fc, c0, w, pu, tmi: lambda e: e.tensor_tensor(
                                out=aT[ab][:, fc, c0:c0 + w], in0=psum[pu][:, 0:w], in1=tmpf[tmi][:, 0:w],
                                op=ALU.mult))(ab, fc, c0, w, pu, tmi),
                                [("ps", pu), ("tmp", tmi)], [("aT", ab, fc, ti)])
                        units.append(unit)
                        first = False
                return units

            def dn_units(si):
                ab = si % 2
                subs = sgs[si]
                nfc = 2 * len(subs)
                units = []
                for (o, (c0, w)) in ([(o, t) for o in range(8) for t in tiles[:2]] +
                                     [(o, t) for o in range(8) for t in tiles[2:]]):
                    def unit(o=o, c0=c0, w=w):
                        ti = c0 // 512
                        pd = next_ps()
                        for fc in range(nfc):
                            idn, wdn = loaded[subs[fc // 2]]
                            mm(psum[pd][:, 0:w], wdn[:, fc % 2, o * 128:(o + 1) * 128], aT[ab][:, fc, c0:c0 + w],
                               fc == 0, fc == nfc - 1, idn + [("aT", ab, fc, ti)], [("ps", pd)])
                        dve((lambda o, c0, w, pd: lambda e: e.scalar_tensor_tensor(
                            out=xT[:, o, c0:c0 + w], in0=psum[pd][:, 0:w], scalar=0.5,
                            in1=xT[:, o, c0:c0 + w], op0=ALU.mult, op1=ALU.add))(o, c0, w, pd),
                            [("ps", pd), ("xT", o, ti)], [("xT", o, ti)])
                        if tstats[0] is not None:
                            tstats[0].chunk_done(o, c0, w)
                    units.append(unit)
                return units

            for u in gu_units(0):
                u()
            for si in range(len(sgs)):
                A = gu_units(si + 1) if si + 1 < len(sgs) else []
                if tail and si == len(sgs) - 1:
                    tstats[0] = TailStats(b)
                Bn = dn_units(si)
                if A:
                    per = -(-len(Bn) // len(A))
                    bi = 0
                    for a in A:
                        a()
                        for _ in range(per):
                            if bi < len(Bn):
                                Bn[bi]()
                                bi += 1
                    while bi < len(Bn):
                        Bn[bi]()
                        bi += 1
                else:
                    for u in Bn:
                        u()
                if si == 1 and mid_hook is not None:
                    mid_hook()
            st["w"] = 0
            return tstats[0].finish() if tstats[0] is not None else None

        def proj_fm(wunit, widx, ncols_chunks, k_n, rhs_buf, rhs_key, tiles, consume):
            for (m, (c0, w)) in ([(m, t) for m in range(ncols_chunks) for t in tiles[:2]] +
                                 [(m, t) for m in range(ncols_chunks) for t in tiles[2:]]):
                if True:
                    ti = c0 // 512
                    pi = next_ps()
                    for k in range(k_n):
                        mm(psum[pi][:, 0:w], wunit[:, k, m * 128:(m + 1) * 128], rhs_buf[:, k, c0:c0 + w],
                           k == 0, k == k_n - 1, widx + [(rhs_key, k, ti)], [("ps", pi)])
                    consume(m, c0, w, ti, pi)

        def build_bias(l):
            T.dma("sp", lambda e: e.dma_start(
                out=cvec[:], in_=bass.AP(relb_t, l * 8 * 257 + 256, [[0, 128], [257, 8]]),
                allow_slow_non_contiguous=True),
                writes=["cvec"], semkey="cvec")
            Bf = Bt[:].rearrange("p a h q -> p (a h q)")
            for j in range(4):
                ty, hh = j // 2, (j % 2) * 4
                tmi = rot("tmp", NT)
                T.dma("sp", (lambda ty, hh, tmi: lambda e: e.dma_start(
                    out=tmpf[tmi][:].rearrange("p (h q) -> p h q", h=4),
                    in_=bass.AP(ext_t, l * 8 * 384 + hh * 384 + 1 + 128 * ty, [[1, 128], [384, 4], [1, 128]])))(ty, hh, tmi),
                    reads=["ext"], writes=[("tmp", tmi)], semkey=("ab", tmi))
                pi = next_ps()
                T.op("pe", (lambda pi, tmi: lambda e: e.matmul(psum[pi][:], lhsT=antiI[:], rhs=tmpf[tmi][:],
                                                              start=True, stop=True))(pi, tmi),
                     reads=[("tmp", tmi), "antiI"], writes=[("ps", pi)])
                act(Bf[:, j * 512:(j + 1) * 512], psum[pi][:], AF.Copy, [("ps", pi)], ["Bt"])
            dve(lambda e: e.memset(Bt[64:128, 0, :, 0:64], NEG), ["Bt"], ["Bt"])

        bias_prebuilt = [False]

        def mixer(b, l, pre=None, tail=False):
            tiles = tiles_of(b)
            ptiles = tiles[:2]
            last = (b == DBG_NBLK - 1)
            aT0k = [("aT", 0, fc, ti) for fc in range(4) for ti in range(3)]
            aT1k = [("aT", 1, fc, ti) for fc in range(4) for ti in range(3)]
            qTk = [("qT", m, ti) for m in range(4) for ti in range(3)]
            yaTk = [("yaT", m, ti) for m in range(4) for ti in range(3)]
            mTk = [("mT", m, ti) for m in range(8) for ti in range(3)]
            T.transfer(aT0k, qTk)
            T.transfer(aT1k, yaTk)
            T.transfer(mTk, ["uT", "yconv"])
            T.transfer([("ycT", c, ti) for c in range(4) for ti in range(3)], [("KT", t) for t in range(8)])
            rmsnorm_to_hT(b, 32 + l * 8, pre)
            if not bias_prebuilt[0]:
                build_bias(l)
            bias_prebuilt[0] = False
            winv = win[l].rearrange("(k p) n -> p k n", p=128)

            def tok_major_rows(wun, iun, hq, outp, outs_, is_v):
                if is_v:
                    tts = range(8)
                elif last:
                    tts = range(4, 8)
                else:
                    tts = []
                for tt in tts:
                    pi = next_ps()
                    for k in range(8):
                        mm(psum[pi][:, 0:256], hT[:, k, tt * 128:(tt + 1) * 128], wun[:, k, :], k == 0, k == 7,
                           iun + [("hT", k, tt // 4)], [("ps", pi)])
                    if is_v:
                        act(V65[:, tt, hq * 4:(hq + 1) * 4, 0:64], psum[pi][:, 0:256].rearrange("p (h d) -> p h d", h=4),
                            AF.Copy, [("ps", pi), ("V65", tt)], [("V65", tt)])
                    if last and tt >= 4:
                        ki = rot("tmp", NT)
                        dve((lambda ki, pi: lambda e: e.tensor_copy(out=kvst[ki][:, 0:256], in_=psum[pi][:, 0:256]))(ki, pi),
                            [("ps", pi), ("tmp", ki)], [("tmp", ki)])
                        T.dma("sp", (lambda ki, tt: lambda e: e.dma_start(
                            out=outp[l, (tt - 4) * 128:(tt - 3) * 128, hq * 256:(hq + 1) * 256], in_=kvst[ki][:, 0:256]))(ki, tt),
                            reads=[("tmp", ki)], semkey=("tmpd", ki))
                if b == 0:
                    pi = next_ps()
                    for k in range(8):
                        mm(psum[pi][0:NS, 0:256], hT[:, k, TB:TB + NS], wun[:, k, :], k == 0, k == 7,
                           iun + [("hT", k, 2)], [("ps", pi)])
                    if is_v:
                        act(V65s[0:NS, 4, hq * 4:(hq + 1) * 4, 0:64], psum[pi][0:NS, 0:256].rearrange("p (h d) -> p h d", h=4),
                            AF.Copy, [("ps", pi), "V65s"], ["V65s"])
                    ki = rot("tmp", NT)
                    dve((lambda ki, pi: lambda e: e.tensor_copy(out=kvst[ki][0:NS, 0:256], in_=psum[pi][0:NS, 0:256]))(ki, pi),
                        [("ps", pi), ("tmp", ki)], [("tmp", ki)])
                    T.dma("sp", (lambda ki: lambda e: e.dma_start(
                        out=outs_[l][:, hq * 256:(hq + 1) * 256], in_=kvst[ki][0:NS, 0:256]))(ki),
                        reads=[("tmp", ki)], semkey=("tmpd", ki))

            for hq in range(2):
                iq, wq = load_w(winv[:, :, 1536 + hq * 256:1536 + (hq + 1) * 256], (8, 256), "q")

                def cons_q(m, c0, w, ti, pi, hq=hq):
                    act(qT[:, hq * 2 + m, c0:c0 + w], psum[pi][:, 0:w], AF.Copy, [("ps", pi)], [("qT", hq * 2 + m, ti)])
                proj_fm(wq, iq, 2, 8, hT, "hT", tiles, cons_q)
            for hq in range(2):
                ik, wk = load_w(winv[:, :, 2048 + hq * 256:2048 + (hq + 1) * 256], (8, 256), "k")

                def cons_k(m, c0, w, ti, pi, hq=hq):
                    mm_ = hq * 2 + m
                    if c0 == TB:
                        act(KTs[:, mm_, 512:512 + NS], psum[pi][:, 0:w], AF.Copy, [("ps", pi)], ["KTs"])
                    else:
                        dve((lambda mm_, c0, w, pi: lambda e: e.tensor_copy(out=KT[:, mm_, c0:c0 + w], in_=psum[pi][:, 0:w]))(mm_, c0, w, pi),
                            [("ps", pi)] + [("KT", c0 // 128 + j) for j in range(4)],
                            [("KT", c0 // 128 + j) for j in range(4)])
                proj_fm(wk, ik, 2, 8, hT, "hT", tiles, cons_k)
                tok_major_rows(wk, ik, hq, okp, oks, False)
            for hq in range(2):
                iv, wv = load_w(winv[:, :, 2560 + hq * 256:2560 + (hq + 1) * 256], (8, 256), "v")
                tok_major_rows(wv, iv, hq, ovp, ovs, True)
            if b == 0:
                for t4 in range(4):
                    T.dma("pool", (lambda t4: lambda e: e.dma_start(
                        out=V65s[:, t4, :, 0:64],
                        in_=cv[l, t4 * 128:(t4 + 1) * 128, :].rearrange("p (h d) -> p h d", h=8)))(t4),
                        reads=["V65s"], writes=["V65s"], semkey="cvs")
                for t4 in range(4):
                    ki = rot("tmp", NT)
                    T.dma("sp", (lambda ki, t4: lambda e: e.dma_start(out=kvst[ki][:], in_=ck[l, t4 * 128:(t4 + 1) * 128, :]))(ki, t4),
                          writes=[("tmp", ki)], semkey=("tmpd", ki))
                    pi = next_ps()
                    for m in range(4):
                        T.op("pe", (lambda pi, ki, m: lambda e: e.transpose(
                            psum[pi][:, m * 128:(m + 1) * 128], kvst[ki][:, m * 128:(m + 1) * 128], ident[:]))(pi, ki, m),
                            reads=[("tmp", ki), "ident"], writes=[("ps", pi)])
                    dve((lambda t4, pi: lambda e: e.tensor_copy(
                        out=KTs[:, :, t4 * 128:(t4 + 1) * 128], in_=psum[pi][:].rearrange("p (m t) -> p m t", m=4)))(t4, pi),
                        [("ps", pi), "KTs"], ["KTs"])

            if MSTAGE <= 3:
                return
            def key_tile(kt_local):
                if kt_local < 0:
                    t = kt_local + 4
                    return (lambda hp: histK[l][:, hp, t * 128:(t + 1) * 128]), ("hK", l), \
                           (lambda h: histV[l][:, t, h, :]), ("hV", l)
                t = kt_local
                return (lambda hp: KT[:, hp, t * 128:(t + 1) * 128]), ("KT", t), \
                       (lambda h: V65[:, t, h, :]), ("V65", t)

            pstate = {}

            def att_A(j, h):
                pglob = b * 8 + j
                nj = min(5, pglob + 1)
                q0 = j * 128
                qti = q0 // 512
                if h == 0:
                    oacc = [next_ps(), next_ps()]
                    pinned.update(oacc)
                    pstate[j] = oacc
                hp, hb = h // 2, (h % 2) * 64
                qap = qT[hb:hb + 64, hp, q0:q0 + 128]
                nnear = min(2, nj)
                nfar = nj - nnear
                pfi = None
                if nfar > 0:
                    pf = next_ps()
                    for jp in range(2, nj):
                        kf, kkey, vf, vkey = key_tile(j - jp)
                        mm(psum[pf][:, (jp - 2) * 128:(jp - 1) * 128], kf(hp)[hb:hb + 64, :], qap, True, True,
                           [kkey, ("qT", hp, qti)], [("ps", pf)])
                pn = next_ps()
                for jp in range(nnear):
                    kf, kkey, vf, vkey = key_tile(j - jp)
                    mm(psum[pn][:, jp * 128:(jp + 1) * 128], kf(hp)[hb:hb + 64, :], qap, True, True,
                       [kkey, ("qT", hp, qti)], [("ps", pn)])
                tmi = rot("tmp", NT)
                dve((lambda pn, tmi, h, nnear: lambda e: e.scalar_tensor_tensor(
                    out=tmpf[tmi][:, 0:nnear * 128].rearrange("p (a q) -> p a q", a=nnear),
                    in0=psum[pn][:, 0:nnear * 128].rearrange("p (a q) -> p a q", a=nnear),
                    scalar=0.125, in1=Bt[:, 0:nnear, h, :], op0=ALU.mult, op1=ALU.add))(pn, tmi, h, nnear),
                    [("ps", pn), "Bt", ("tmp", tmi)], [("tmp", tmi)])
                if nfar > 0:
                    pfi = rot("pf", 4)
                    if nfar == 3:
                        act(Pfar[pfi][:, 0:384], psum[pf][:, 0:384], AF.Exp, [("ps", pf), ("pf", pfi), "cvec"], [("pf", pfi)],
                            scale=0.125, bias=cvec[:, h:h + 1])
                        dve((lambda pfi: lambda e: e.memset(Pfar[pfi][0:64, 320:384], 0.0))(pfi), [("pf", pfi)], [("pf", pfi)])
                    else:
                        act(Pfar[pfi][:, 0:nfar * 128], psum[pf][:, 0:nfar * 128], AF.Exp,
                            [("ps", pf), ("pf", pfi), "cvec"], [("pf", pfi)], scale=0.125, bias=cvec[:, h:h + 1])
                pni = rot("pn", 2)
                act(Pnear[pni][:, 0:nnear * 128], tmpf[tmi][:, 0:nnear * 128], AF.Exp,
                    [("tmp", tmi), ("pn", pni)], [("pn", pni)])
                return (nj, pni, pfi)

            def att_B(j, h, info):
                nj, pni, pfi = info
                ob = pstate[j][h // 4]
                osl = psum[ob][:, (h % 4) * 65:(h % 4) * 65 + 65]
                order = list(range(2, nj)) + list(range(min(2, nj)))
                for oi_, jp in enumerate(order):
                    kf, kkey, vf, vkey = key_tile(j - jp)
                    if jp < 2:
                        lhs = Pnear[pni][:, jp * 128:(jp + 1) * 128]
                        rk = ("pn", pni)
                    else:
                        lhs = Pfar[pfi][:, (jp - 2) * 128:(jp - 1) * 128]
                        rk = ("pf", pfi)
                    mm(osl, lhs, vf(h), oi_ == 0, oi_ == nj - 1, [rk, vkey], [("ps", ob)])

            def att_finish(j):
                oacc = pstate[j]
                q0 = j * 128
                qti = q0 // 512
                yi = rot("tmp", NT)
                for half in range(2):
                    ob = oacc[half]
                    o3 = psum[ob][:, 0:260].rearrange("p (h e) -> p h e", h=4)
                    dve((lambda o3, half: lambda e: e.reciprocal(out=rden[:, half * 4:half * 4 + 4], in_=o3[:, :, 64]))(o3, half),
                        [("ps", ob), "rden"], ["rden"])
                    dve((lambda o3, half, yi: lambda e: e.tensor_tensor(
                        out=yatok[yi][:, half * 256:(half + 1) * 256].rearrange("p (h d) -> p h d", h=4),
                        in0=o3[:, :, 0:64],
                        in1=rden[:, half * 4:half * 4 + 4].unsqueeze(2).to_broadcast([128, 4, 64]),
                        op=ALU.mult))(o3, half, yi),
                        [("ps", ob), "rden", ("tmp", yi)], [("tmp", yi)])
                pinned.difference_update(oacc)

                def tr():
                    pt = next_ps()
                    for c in range(4):
                        T.op("pe", (lambda pt, c, yi: lambda e: e.transpose(
                            psum[pt][:, c * 128:(c + 1) * 128], yatok[yi][:, c * 128:(c + 1) * 128], ident[:]))(pt, c, yi),
                            reads=[("tmp", yi), "ident"], writes=[("ps", pt)])
                    act(yaT[:, :, q0:q0 + 128], psum[pt][:].rearrange("p (c q) -> p c q", c=4), AF.Copy,
                        [("ps", pt)] + [("yaT", c, qti) for c in range(4)], [("yaT", c, qti) for c in range(4)])
                return tr

            units = [(j, h) for j in range(8) for h in range(8)]
            pending = []
            infoA = att_A(*units[0])
            for ui, (j, h) in enumerate(units):
                nxt = att_A(*units[ui + 1]) if ui + 1 < len(units) else None
                for item in pending:
                    item[0] -= 1
                for item in [it for it in pending if it[0] <= 0]:
                    item[1]()
                    pending.remove(item)
                att_B(j, h, infoA)
                if h == 7:
                    pending.append([3, att_finish(j)])
                infoA = nxt
            for item in pending:
                item[1]()

            if MSTAGE <= 4:
                return
            if b == 0:
                oacc = [next_ps(), next_ps()]
                pinned.update(oacc)
                def s_A(h):
                    hp, hb = h // 2, (h % 2) * 64
                    qap = qT[hb:hb + 64, hp, TB:TB + NS]
                    pf = next_ps()
                    for r in range(4):
                        mm(psum[pf][:, r * NS:(r + 1) * NS], KTs[hb:hb + 64, hp, r * 128:(r + 1) * 128], qap, True, True,
                           ["KTs", ("qT", hp, 2)], [("ps", pf)])
                    mm(psum[pf][0:NS, 64:64 + NS], KTs[hb:hb + 64, hp, 512:512 + NS], qap, True, True,
                       ["KTs", ("qT", hp, 2)], [("ps", pf)])
                    pni = rot("pn", 2)
                    act(Pnear[pni][:, 0:48], psum[pf][:, 0:48], AF.Exp, [("ps", pf), ("pn", pni), "cvec"], [("pn", pni)],
                        scale=0.125, bias=cvec[:, h:h + 1])
                    tmi = rot("tmp", NT)
                    dve((lambda pf, tmi, h: lambda e: e.scalar_tensor_tensor(
                        out=tmpf[tmi][:, 48:64], in0=psum[pf][:, 48:64], scalar=0.125, in1=Bt[:, 1, h, 0:NS],
                        op0=ALU.mult, op1=ALU.add))(pf, tmi, h), [("ps", pf), "Bt", ("tmp", tmi)], [("tmp", tmi)])
                    dve((lambda pf, tmi, h: lambda e: e.scalar_tensor_tensor(
                        out=tmpf[tmi][0:NS, 64:80], in0=psum[pf][0:NS, 64:80], scalar=0.125, in1=Bt[0:NS, 0, h, 0:NS],
                        op0=ALU.mult, op1=ALU.add))(pf, tmi, h), [("ps", pf), "Bt", ("tmp", tmi)], [("tmp", tmi)])
                    act(Pnear[pni][:, 48:64], tmpf[tmi][:, 48:64], AF.Exp, [("tmp", tmi), ("pn", pni)], [("pn", pni)])
                    act(Pnear[pni][0:NS, 64:80], tmpf[tmi][0:NS, 64:80], AF.Exp, [("tmp", tmi), ("pn", pni)], [("pn", pni)])
                    return pni

                def s_B(h, pni):
                    ob = oacc[h // 4]
                    osl = psum[ob][0:NS, (h % 4) * 65:(h % 4) * 65 + 65]
                    for r in range(4):
                        mm(osl, Pnear[pni][:, r * NS:(r + 1) * NS], V65s[:, r, h, :], r == 0, False,
                           [("pn", pni), "V65s"], [("ps", ob)])
                    mm(osl, Pnear[pni][0:NS, 64:80], V65s[0:NS, 4, h, :], False, True,
                       [("pn", pni), "V65s"], [("ps", ob)])

                cur = s_A(0)
                for h in range(8):
                    nxt = s_A(h + 1) if h + 1 < 8 else None
                    s_B(h, cur)
                    cur = nxt
                yi = rot("tmp", NT)
                for half in range(2):
                    ob = oacc[half]
                    o3 = psum[ob][0:NS, 0:260].rearrange("p (h e) -> p h e", h=4)
                    dve((lambda o3, half: lambda e: e.reciprocal(out=rden[0:NS, half * 4:half * 4 + 4], in_=o3[:, :, 64]))(o3, half),
                        [("ps", ob), "rden"], ["rden"])
                    dve((lambda o3, half, yi: lambda e: e.tensor_tensor(
                        out=yatok[yi][0:NS, half * 256:(half + 1) * 256].rearrange("p (h d) -> p h d", h=4),
                        in0=o3[:, :, 0:64],
                        in1=rden[0:NS, half * 4:half * 4 + 4].unsqueeze(2).to_broadcast([NS, 4, 64]),
                        op=ALU.mult))(o3, half, yi),
                        [("ps", ob), "rden", ("tmp", yi)], [("tmp", yi)])
                pinned.difference_update(oacc)
                pt = next_ps()
                for c in range(4):
                    T.op("pe", (lambda pt, c, yi: lambda e: e.transpose(
                        psum[pt][:, c * NS:(c + 1) * NS], yatok[yi][0:NS, c * 128:(c + 1) * 128], ident[0:NS, 0:NS]))(pt, c, yi),
                        reads=[("tmp", yi), "ident"], writes=[("ps", pt)])
                act(yaT[:, :, TB:TB + NS], psum[pt][:, 0:4 * NS].rearrange("p (c q) -> p c q", c=4), AF.Copy,
                    [("ps", pt)] + [("yaT", c, 2) for c in range(4)], [("yaT", c, 2) for c in range(4)])

            if MSTAGE <= 5:
                return
            if not last:
                dve(lambda e: e.tensor_copy(out=histK[l][:], in_=KT[:, :, 512:1024]),
                    [("KT", t) for t in range(4, 8)] + [("hK", l)], [("hK", l)])
                dve(lambda e: e.tensor_copy(out=histV[l][:], in_=V65[:, 4:8, :, :]),
                    [("V65", t) for t in range(4, 8)] + [("hV", l)], [("hV", l)])

            if MSTAGE <= 6:
                return
            KTk = [("KT", t) for t in range(8)]
            ycTk = [("ycT", c, ti) for c in range(4) for ti in range(3)]
            T.transfer(KTk, ycTk)
            wc0 = (l * 3) * 4
            for hc in range(2):
                icc, wcc = load_w(winv[:, :, 512 + hc * 256:512 + (hc + 1) * 256], (8, 256), "cc")
                icv, wcv = load_w(winv[:, :, 1024 + hc * 256:1024 + (hc + 1) * 256], (8, 256), "cv")
                icb, wcb = load_w(winv[:, :, hc * 256:(hc + 1) * 256], (8, 256), "cb")
                cs = (2 * hc, 2 * hc + 1)
                for (c0, w) in tiles:
                    ti = c0 // 512
                    is_s = (c0 == TB)
                    if is_s:
                        for c in cs:
                            T.dma("sp", (lambda c: lambda e: e.dma_start(
                                out=uT[:, c, 0:2], in_=sc[l][:, c * 128:(c + 1) * 128].rearrange("j p -> p j"),
                                allow_slow_non_contiguous=True))(c),
                                writes=["uT"], reads=["uT"], semkey="uhist")
                    else:
                        dve((lambda hc: lambda e: e.tensor_copy(out=uT[:, 2 * hc:2 * hc + 2, 0:2], in_=histU[l][:, 2 * hc:2 * hc + 2, :]))(hc),
                            [("hU", l), "uT"], ["uT"])
                    for c in cs:
                        cl = c - 2 * hc
                        pc = next_ps()
                        for k in range(8):
                            mm(psum[pc][:, 0:w], wcc[:, k, cl * 128:(cl + 1) * 128], hT[:, k, c0:c0 + w],
                               k == 0, k == 7, icc + [("hT", k, ti)], [("ps", pc)])
                        tmi = rot("tmp", NT)
                        act(tmpf[tmi][:, 0:w], psum[pc][:, 0:w], AF.Copy, [("ps", pc)], [("tmp", tmi)])
                        pv = next_ps()
                        for k in range(8):
                            mm(psum[pv][:, 0:w], wcv[:, k, cl * 128:(cl + 1) * 128], hT[:, k, c0:c0 + w],
                               k == 0, k == 7, icv + [("hT", k, ti)], [("ps", pv)])
                        dve((lambda c, w, pv, tmi: lambda e: e.tensor_tensor(
                            out=uT[:, c, 2:2 + w], in0=psum[pv][:, 0:w], in1=tmpf[tmi][:, 0:w], op=ALU.mult))(c, w, pv, tmi),
                            [("ps", pv), ("tmp", tmi), "uT"], ["uT"])
                        dve((lambda c, w: lambda e: e.tensor_scalar(
                            out=yconv[:, c, 0:w], in0=uT[:, c, 0:w], scalar1=wcT[:, wc0 + c: wc0 + c + 1],
                            scalar2=None, op0=ALU.mult))(c, w), ["uT", "wcT", "yconv"], ["yconv"])
                        for j in (1, 2):
                            dve((lambda c, w, j: lambda e: e.scalar_tensor_tensor(
                                out=yconv[:, c, 0:w], in0=uT[:, c, j:j + w],
                                scalar=wcT[:, wc0 + j * 4 + c: wc0 + j * 4 + c + 1],
                                in1=yconv[:, c, 0:w], op0=ALU.mult, op1=ALU.add))(c, w, j),
                                ["uT", "yconv", "wcT"], ["yconv"])
                        pb = next_ps()
                        for k in range(8):
                            mm(psum[pb][:, 0:w], wcb[:, k, cl * 128:(cl + 1) * 128], hT[:, k, c0:c0 + w],
                               k == 0, k == 7, icb + [("hT", k, ti)], [("ps", pb)])
                        dve((lambda c, c0, w, pb: lambda e: e.tensor_tensor(
                            out=ycT[:, c, c0:c0 + w], in0=psum[pb][:, 0:w], in1=yconv[:, c, 0:w], op=ALU.mult))(c, c0, w, pb),
                            [("ps", pb), "yconv"], [("ycT", c, ti)])
                    if is_s:
                        dve((lambda hc: lambda e: e.tensor_copy(out=ust[:, 2 * hc:2 * hc + 2, :], in_=uT[:, 2 * hc:2 * hc + 2, NS:NS + 2]))(hc),
                            ["uT", "ust"], ["ust"])
                        for c in cs:
                            T.dma("sp", (lambda c: lambda e: e.dma_start(out=ocs[l][:, c * 128:(c + 1) * 128].rearrange("j p -> p j"), in_=ust[:, c, :], allow_slow_non_contiguous=True))(c),
                                  reads=["ust"], semkey="ust")
                    else:
                        dve((lambda hc: lambda e: e.tensor_copy(out=histU[l][:, 2 * hc:2 * hc + 2, :], in_=uT[:, 2 * hc:2 * hc + 2, 512:514]))(hc),
                            ["uT", ("hU", l)], [("hU", l)])
                        if last and c0 == 512:
                            dve((lambda hc: lambda e: e.tensor_copy(out=ust[:, 2 * hc:2 * hc + 2, :], in_=uT[:, 2 * hc:2 * hc + 2, 512:514]))(hc),
                                ["uT", "ust"], ["ust"])
                            for c in cs:
                                T.dma("sp", (lambda c: lambda e: e.dma_start(out=ocp[l][:, c * 128:(c + 1) * 128].rearrange("j p -> p j"), in_=ust[:, c, :], allow_slow_non_contiguous=True))(c),
                                      reads=["ust"], semkey="ust")

            if MSTAGE <= 7:
                return
            T.transfer(["uT", "yconv"], mTk)
            for og in range(4):
                iwco, wcou = load_w(wco[l].rearrange("(c p) n -> p c n", p=128)[:, :, og * 256:(og + 1) * 256], (4, 256), "wco")
                iwao, waou = load_w(wao[l].rearrange("(c p) n -> p c n", p=128)[:, :, og * 256:(og + 1) * 256], (4, 256), "wao")
                igc, wgc = load_w(winv[:, :, 3072 + og * 256: 3072 + (og + 1) * 256], (8, 256), "gc")
                iga, wga = load_w(winv[:, :, 4096 + og * 256: 4096 + (og + 1) * 256], (8, 256), "ga")
                for (oo, (c0, w)) in ([(oo, t) for oo in range(2) for t in tiles[:2]] +
                                      [(oo, t) for oo in range(2) for t in tiles[2:]]):
                    o = og * 2 + oo
                    if True:
                        ti = c0 // 512
                        res = []
                        for (wg_, ig_, wp_, ip_, src, skey, bcol) in (
                                (wgc, igc, wcou, iwco, ycT, "ycT", l * 16 + o),
                                (wga, iga, waou, iwao, yaT, "yaT", l * 16 + 8 + o)):
                            pg = next_ps()
                            for k in range(8):
                                mm(psum[pg][:, 0:w], wg_[:, k, oo * 128:(oo + 1) * 128], hT[:, k, c0:c0 + w],
                                   k == 0, k == 7, ig_ + [("hT", k, ti)], [("ps", pg)])
                            tmi = rot("tmp", NT)
                            act(tmpf[tmi][:, 0:w], psum[pg][:, 0:w], AF.Sigmoid, [("ps", pg), ("tmp", tmi), "bgT"], [("tmp", tmi)],
                                bias=bgT[:, bcol:bcol + 1])
                            pp = next_ps()
                            for c in range(4):
                                mm(psum[pp][:, 0:w], wp_[:, c, oo * 128:(oo + 1) * 128], src[:, c, c0:c0 + w],
                                   c == 0, c == 3, ip_ + [(skey, c, ti)], [("ps", pp)])
                            dve((lambda tmi, pp, w: lambda e: e.tensor_tensor(
                                out=tmpf[tmi][:, 0:w], in0=psum[pp][:, 0:w], in1=tmpf[tmi][:, 0:w], op=ALU.mult))(tmi, pp, w),
                                [("ps", pp), ("tmp", tmi)], [("tmp", tmi)])
                            res.append(tmi)
                        dve((lambda o, c0, w, r0, r1: lambda e: e.tensor_tensor(
                            out=mT[:, o, c0:c0 + w], in0=tmpf[r0][:, 0:w], in1=tmpf[r1][:, 0:w], op=ALU.add))(o, c0, w, res[0], res[1]),
                            [("tmp", res[0]), ("tmp", res[1])], [("mT", o, ti)])

            if MSTAGE <= 8:
                return
            tst = TailStats(b) if tail else None
            for og in range(4):
                iwo, wou = load_w(wo[l].rearrange("(k p) n -> p k n", p=128)[:, :, og * 256:(og + 1) * 256], (8, 256), "wo")

                def cons_o(m, c0, w, ti, pi, og=og):
                    o = og * 2 + m
                    dve((lambda o, c0, w, pi: lambda e: e.tensor_tensor(
                        out=xT[:, o, c0:c0 + w], in0=psum[pi][:, 0:w], in1=xT[:, o, c0:c0 + w], op=ALU.add))(o, c0, w, pi),
                        [("ps", pi), ("xT", o, ti)], [("xT", o, ti)])
                    if tst is not None:
                        tst.chunk_done(o, c0, w)
                proj_fm(wou, iwo, 2, 8, mT, "mT", tiles, cons_o)
            return tst.finish() if tst is not None else None

        def debug_dump():
            for (buf, key, nch, row0) in ((ycT, "ycT", 4, 1024), (yaT, "yaT", 4, 1536), (mT, "mT", 8, 2048)):
                T.dma("pool", (lambda buf, nch, row0: lambda e: e.dma_start(
                    out=yp[row0:row0 + nch * 128, :].rearrange("(c p) t -> p c t", p=128),
                    in_=buf[:, 0:nch, 0:1024]))(buf, nch, row0),
                    reads=[(key, c, ti) for c in range(nch) for ti in range(3)], semkey="dbgdump")

        def final_store(b, pre=None):
            rmsnorm_final(b, pre)

        def rmsnorm_final(b, pre=None):
            for (c0, w) in tiles_of(b):
                ti = c0 // 512
                ri = pre[ti] if pre is not None else rms_stats(c0, w, ti)
                for k in range(8):
                    dve((lambda k, c0, w, ri: lambda e: e.scalar_tensor_tensor(
                        out=xT[:, k, c0:c0 + w], in0=xT[:, k, c0:c0 + w],
                        scalar=gains[:, 96 + k: 97 + k], in1=tmpf[ri][:, 0:w],
                        op0=ALU.mult, op1=ALU.mult))(k, c0, w, ri),
                        [("xT", k, ti), ("tmp", ri), "gains"], [("xT", k, ti)])
                nsub = (w + 127) // 128
                for s_ in range(nsub):
                    n = min(128, w - s_ * 128)
                    col = c0 + s_ * 128
                    for half in range(2):
                        oi = rot("xst", 4)
                        pi = next_ps()
                        for jj in range(4):
                            k = half * 4 + jj
                            T.op("pe", (lambda pi, jj, k, col, n: lambda e: e.transpose(
                                psum[pi][0:n, jj * 128:(jj + 1) * 128], xT[:, k, col:col + n], ident[:]))(pi, jj, k, col, n),
                                reads=[("xT", k, ti), "ident"], writes=[("ps", pi)])
                        if half == 0:
                            dve((lambda oi, pi, n: lambda e: e.tensor_copy(out=ostage[oi][0:n, :], in_=psum[pi][0:n, :]))(oi, pi, n),
                                [("ps", pi), ("xst", oi)], [("xst", oi)])
                        else:
                            act(ostage[oi][0:n, :], psum[pi][0:n, :], AF.Copy, [("ps", pi), ("xst", oi)], [("xst", oi)])
                        if c0 == TB:
                            dst = ys[:, half * 512:(half + 1) * 512]
                        else:
                            dst = yp[b * TB + col: b * TB + col + n, half * 512:(half + 1) * 512]
                        T.dma("sp", (lambda oi, dst, n: lambda e: e.dma_start(out=dst, in_=ostage[oi][0:n, :]))(oi, dst, n),
                              reads=[("xst", oi)], semkey=("xst", oi))

        STAGE = int(os.environ.get("K_STAGE", 9))
        for b in range(DBG_NBLK):
            if STAGE >= 1:
                load_x_block(b)
            pre = None
            full = STAGE >= 5
            for l in range(DBG_NLAY):
                if STAGE >= 2:
                    def hook(l=l):
                        build_bias(l)
                        bias_prebuilt[0] = True
                    pre = ffn(b, l, w1gu, w1d, 0 + l * 8, pre=pre, tail=full, mid_hook=hook if STAGE >= 3 else None)
                if STAGE >= 3:
                    pre = mixer(b, l, pre=pre, tail=full)
                    if os.environ.get("K_DUMP"):
                        debug_dump()
                if STAGE >= 4:
                    pre = ffn(b, l, w2gu, w2d, 64 + l * 8, pre=pre, tail=full)
            if STAGE >= 5 or os.environ.get("K_FINAL"):
                final_store(b, pre)
        T.final_wait("sp")

        with nc.Block() as block:
            @block.tensor
            def _(e):
                T.replay("pe", e)

            @block.scalar
            def _(e):
                T.replay("act", e)

            @block.vector
            def _(e):
                T.replay("dve", e)

            @block.gpsimd
            def _(e):
                T.replay("pool", e)

            @block.sync
            def _(e):
                T.replay("sp", e)
        print("kernel: %d instructions, sems=%d" % (T.ninstr, len(T.dsem) + 5))
    return nc


_NC_CACHE = {}


def kernel(x_prompt, x_sample, cache_k, cache_v, state_conv,
           norm_ffn1, w_ffn1_gu, w_ffn1_down, norm_mix, w_in, b_gate, w_conv,
           rel_bias, w_conv_out, w_attn_out, w_o, norm_ffn2, w_ffn2_gu, w_ffn2_down,
           norm_final):
    f = lambda a: np.ascontiguousarray(np.asarray(a, dtype=np.float32))
    if "nc" not in _NC_CACHE:
        _NC_CACHE["nc"] = build_program()
    nc = _NC_CACHE["nc"]
    shared = {
        "norm_ffn1": f(norm_ffn1), "w_ffn1_gu": f(w_ffn1_gu), "w_ffn1_down": f(w_ffn1_down),
        "norm_mix": f(norm_mix), "w_in": f(w_in), "b_gate": f(b_gate), "w_conv": f(w_conv),
        "rel_bias": f(rel_bias), "w_conv_out": f(w_conv_out), "w_attn_out": f(w_attn_out),
        "w_o": f(w_o), "norm_ffn2": f(norm_ffn2), "w_ffn2_gu": f(w_ffn2_gu),
        "w_ffn2_down": f(w_ffn2_down), "norm_final": f(norm_final),
    }
    x_prompt = np.asarray(x_prompt); x_sample = np.asarray(x_sample)
    cache_k = np.asarray(cache_k); cache_v = np.asarray(cache_v); state_conv = np.asarray(state_conv)
    in_maps = []
    for c in range(8):
        m = dict(shared)
        m["xp"] = f(x_prompt[c])
        m["xs"] = f(x_sample[c])
        m["ck"] = f(cache_k[:, c].reshape(DEPTH, 512, 512))
        m["cv"] = f(cache_v[:, c].reshape(DEPTH, 512, 512))
        m["sc"] = f(state_conv[:, c])
        in_maps.append(m)
    res = run_bass_kernel_spmd(nc, in_maps, core_ids=list(range(8)))
    R = res.results
    y_prompt = np.stack([R[c]["yp"] for c in range(8)], 0).astype(np.float32)
    y_sample = np.stack([R[c]["ys"] for c in range(8)], 0).astype(np.float32)
    nkp = np.stack([R[c]["okp"].reshape(DEPTH, 512, 8, 64) for c in range(8)], 1).astype(np.float32)
    nvp = np.stack([R[c]["ovp"].reshape(DEPTH, 512, 8, 64) for c in range(8)], 1).astype(np.float32)
    ncp = np.stack([R[c]["ocp"] for c in range(8)], 1).astype(np.float32)
    nks = np.stack([R[c]["oks"].reshape(DEPTH, NS, 8, 64) for c in range(8)], 1).astype(np.float32)
    nvs = np.stack([R[c]["ovs"].reshape(DEPTH, NS, 8, 64) for c in range(8)], 1).astype(np.float32)
    ncs = np.stack([R[c]["ocs"] for c in range(8)], 1).astype(np.float32)
    return (y_prompt, y_sample, nkp, nvp, ncp, nks, nvs, ncs)
```
